# Optimizing a Trainium2 kernel written in Bass

```python
import jax, jax.numpy as jnp
from jax import lax
import numpy as np

D_MODEL = 1024
BATCH = 8
SEQ = 8192
DEPTH = 2

GRID_W = 64
CTX_LEN = 256
HEAD_DIM = 64
ATTN_WIDTH = D_MODEL // 2
N_HEADS = ATTN_WIDTH // HEAD_DIM
N_KV_HEADS = N_HEADS // 4
GQA_GROUP = N_HEADS // N_KV_HEADS
KV_WIDTH = N_KV_HEADS * HEAD_DIM
ROPE_PAIRS = HEAD_DIM // 4
ROPE_THETA = 10000.0
Q_BLOCK = 128
ATTN_SCALE = HEAD_DIM ** -0.5
LRU_WIDTH = D_MODEL // 4
LRU_BLOCKS = 4
LRU_BLOCK = LRU_WIDTH // LRU_BLOCKS
LRU_CONV_W = 4
LRU_PAD = (2, 1)
LRU_C = 8.0
SC_WIDTH = D_MODEL // 4
SC_CONV_W = 3
SC_PAD = (1, 1)
MIX_WIDTH = ATTN_WIDTH + LRU_WIDTH + SC_WIDTH
IN_WIDTH = ATTN_WIDTH + 2 * KV_WIDTH + 2 * LRU_WIDTH + 3 * SC_WIDTH
D_FF = (8 * D_MODEL // 3 + 127) // 128 * 128
N_MOD = 9
EPS = 1e-6

kernel_name = 'hymba_style_diffusion_hybrid_block'


def _rms(x):
    xf = x.astype(jnp.float32)
    return (xf * lax.rsqrt(jnp.mean(xf * xf, axis=-1, keepdims=True) + EPS)).astype(x.dtype)


def rms_norm(x, g):
    return _rms(x) * g


def ada_norm(x, g, shift, scale):
    return rms_norm(x, g) * (1 + scale[:, None, :]) + shift[:, None, :]


def swiglu(h, w_in, w_out):
    gate, up = jnp.split(h @ w_in, 2, axis=-1)
    return (jax.nn.silu(gate) * up) @ w_out


def dw_conv(x, w, b, pad):
    y = lax.conv_general_dilated(x, w[:, None, :], (1,), [pad],
                                 dimension_numbers=('NWC', 'WIO', 'NWC'),
                                 feature_group_count=x.shape[-1])
    return y + b


def axial_rope_tables(seq):
    rows = seq // GRID_W
    row_ids = jnp.repeat(jnp.arange(rows), GRID_W).astype(jnp.float32)
    col_ids = jnp.tile(jnp.arange(GRID_W), rows).astype(jnp.float32)
    inv_freq = ROPE_THETA ** (-jnp.arange(ROPE_PAIRS, dtype=jnp.float32) / ROPE_PAIRS)
    ang = jnp.stack([row_ids[:, None] * inv_freq, col_ids[:, None] * inv_freq], axis=1)
    return jnp.cos(ang), jnp.sin(ang)


def apply_rope(x, cos, sin):
    b, s, h, _ = x.shape
    xr = x.astype(jnp.float32).reshape(b, s, h, 2, 2, ROPE_PAIRS)
    x1, x2 = xr[..., 0, :], xr[..., 1, :]
    cs, sn = cos[None, :, None], sin[None, :, None]
    out = jnp.stack([x1 * cs - x2 * sn, x2 * cs + x1 * sn], axis=-2)
    return out.reshape(b, s, h, HEAD_DIM).astype(x.dtype)


def head_rms(x, g):
    return rms_norm(x.reshape(*x.shape[:-1], -1, HEAD_DIM), g)


def attend(q, k, v):
    s = jnp.einsum('bqkgd,bskd->bkgqs', q, k, preferred_element_type=jnp.float32) * ATTN_SCALE
    p = jax.nn.softmax(s, axis=-1).astype(v.dtype)
    return jnp.einsum('bkgqs,bskd->bqkgd', p, v)


def blocked_attention(q, k, v):
    b, s = q.shape[:2]
    nb = s // Q_BLOCK
    qb = q.reshape(b, nb, Q_BLOCK, N_KV_HEADS, GQA_GROUP, HEAD_DIM).swapaxes(0, 1)
    o = lax.map(lambda qblk: attend(qblk, k, v), qb)
    return o.swapaxes(0, 1).reshape(b, s, ATTN_WIDTH)


def _lin_combine(left, right):
    a1, b1 = left
    a2, b2 = right
    return a1 * a2, a2 * b1 + b2


def lru_scan(u, wa, ba, wx, bx, lam, h0, reverse):
    b, l, w = u.shape
    ub = u.reshape(b, l, LRU_BLOCKS, LRU_BLOCK)
    r = jax.nn.sigmoid((jnp.einsum('blnd,nde->blne', ub, wa).reshape(b, l, w) + ba).astype(jnp.float32))
    i = jax.nn.sigmoid((jnp.einsum('blnd,nde->blne', ub, wx).reshape(b, l, w) + bx).astype(jnp.float32))
    log_a = -LRU_C * r * jax.nn.softplus(-lam.astype(jnp.float32))
    a = jnp.exp(log_a)
    xin = jnp.sqrt(-jnp.expm1(2 * log_a)) * i * u.astype(jnp.float32)
    edge = l - 1 if reverse else 0
    xin = xin.at[:, edge].add(a[:, edge] * h0)
    _, h = lax.associative_scan(_lin_combine, (a, xin), reverse=reverse, axis=1)
    return h


def lru_dir(u, p, d, h0, reverse):
    return lru_scan(u, p['wa'][d], p['ba'][d], p['wx'][d], p['bx'][d], p['lam'][d], h0, reverse)


def merge_groups(parts, g, w_out):
    return jnp.concatenate([_rms(t) for t in parts], axis=-1) * g @ w_out


def mixer(hx, hc, p, cos, sin, with_ctx):
    bsz, seq, _ = hx.shape
    n_ctx = hc.shape[1]
    cuts = [ATTN_WIDTH, ATTN_WIDTH + KV_WIDTH, ATTN_WIDTH + 2 * KV_WIDTH]
    cuts += [cuts[-1] + LRU_WIDTH, cuts[-1] + 2 * LRU_WIDTH]
    cuts += [cuts[-1] + SC_WIDTH, cuts[-1] + 2 * SC_WIDTH]
    qx, kx, vx, ux, gx, bgx, cgx, sx = jnp.split(hx @ p['w_in'], cuts, axis=-1)
    qc, kc, vc, uc, gc, bgc, cgc, sc = jnp.split(hc @ p['w_in'], cuts, axis=-1)

    qx = apply_rope(head_rms(qx, p['q_g']), cos, sin).reshape(bsz, seq, N_KV_HEADS, GQA_GROUP, HEAD_DIM)
    kx = apply_rope(head_rms(kx, p['k_g']), cos, sin)
    kc = head_rms(kc, p['k_g'])
    vx = vx.reshape(bsz, seq, N_KV_HEADS, HEAD_DIM)
    vc = vc.reshape(bsz, n_ctx, N_KV_HEADS, HEAD_DIM)
    k_all = jnp.concatenate([kc, kx], axis=1)
    v_all = jnp.concatenate([vc, vx], axis=1)
    attn_x = blocked_attention(qx, k_all, v_all)

    ux = dw_conv(ux, p['lru_conv_w'], p['lru_conv_b'], LRU_PAD)
    uc = dw_conv(uc, p['lru_conv_w'], p['lru_conv_b'], LRU_PAD)
    h0 = jnp.zeros((bsz, LRU_WIDTH), jnp.float32)
    hc_f = lru_dir(uc, p, 0, h0, False)
    hc_b = lru_dir(uc, p, 1, h0, True)
    hx_f = lru_dir(ux, p, 0, hc_f[:, -1], False)
    hx_b = lru_dir(ux, p, 1, hc_b[:, 0], True)
    lru_x = (hx_f + hx_b).astype(hx.dtype) * jax.nn.gelu(gx)

    sc_x = bgx * dw_conv(cgx * sx, p['sc_conv_w'], p['sc_conv_b'], SC_PAD)

    out_x = merge_groups([attn_x, lru_x, sc_x], p['grp_g'], p['w_out'])
    if not with_ctx:
        return out_x, None

    qc = head_rms(qc, p['q_g']).reshape(bsz, n_ctx, N_KV_HEADS, GQA_GROUP, HEAD_DIM)
    attn_c = attend(qc, kc, vc).reshape(bsz, n_ctx, ATTN_WIDTH)
    lru_c = (hc_f + hc_b).astype(hc.dtype) * jax.nn.gelu(gc)
    sc_c = bgc * dw_conv(cgc * sc, p['sc_conv_w'], p['sc_conv_b'], SC_PAD)
    out_c = merge_groups([attn_c, lru_c, sc_c], p['grp_g'], p['w_out'])
    return out_x, out_c


def setup_inputs(seed: int = 0) -> dict:
    key = jax.random.key(seed)
    ks = jax.random.split(key, 24)
    f32 = jnp.float32

    def nrm(k, shape, scale):
        return jax.random.normal(k, shape, f32) * scale

    a_pow = jax.random.uniform(ks[18], (DEPTH, 2, LRU_WIDTH), f32, 0.9, 0.999)
    a_base = a_pow ** (1.0 / LRU_C)
    return {
        'x': nrm(ks[0], (BATCH, SEQ, D_MODEL), 1.0),
        'c': nrm(ks[1], (BATCH, D_MODEL), 1.0),
        'ctx': nrm(ks[2], (BATCH, CTX_LEN, D_MODEL), 1.0),
        'c_ctx': nrm(ks[3], (D_MODEL,), 1.0),
        'w_mod': nrm(ks[4], (DEPTH, D_MODEL, N_MOD * D_MODEL), 0.5 * D_MODEL ** -0.5),
        'b_mod': nrm(ks[5], (DEPTH, N_MOD * D_MODEL), 0.02),
        'norm_g': 1.0 + nrm(ks[6], (DEPTH, 3, D_MODEL), 0.02),
        'w_ffn_in': nrm(ks[7], (DEPTH, 2, D_MODEL, 2 * D_FF), D_MODEL ** -0.5),
        'w_ffn_out': nrm(ks[8], (DEPTH, 2, D_FF, D_MODEL), D_FF ** -0.5),
        'w_in': nrm(ks[9], (DEPTH, D_MODEL, IN_WIDTH), D_MODEL ** -0.5),
        'q_norm_g': 1.0 + nrm(ks[10], (DEPTH, HEAD_DIM), 0.02),
        'k_norm_g': 1.0 + nrm(ks[11], (DEPTH, HEAD_DIM), 0.02),
        'lru_conv_w': nrm(ks[12], (DEPTH, LRU_CONV_W, LRU_WIDTH), LRU_CONV_W ** -0.5),
        'lru_conv_b': nrm(ks[13], (DEPTH, LRU_WIDTH), 0.02),
        'lru_wa': nrm(ks[14], (DEPTH, 2, LRU_BLOCKS, LRU_BLOCK, LRU_BLOCK), LRU_BLOCK ** -0.5),
        'lru_ba': nrm(ks[15], (DEPTH, 2, LRU_WIDTH), 0.02),
        'lru_wx': nrm(ks[16], (DEPTH, 2, LRU_BLOCKS, LRU_BLOCK, LRU_BLOCK), LRU_BLOCK ** -0.5),
        'lru_bx': nrm(ks[17], (DEPTH, 2, LRU_WIDTH), 0.02),
        'lru_lambda': jnp.log(a_base) - jnp.log1p(-a_base),
        'sc_conv_w': nrm(ks[19], (DEPTH, SC_CONV_W, SC_WIDTH), SC_CONV_W ** -0.5),
        'sc_conv_b': nrm(ks[20], (DEPTH, SC_WIDTH), 0.02),
        'grp_norm_g': 1.0 + nrm(ks[21], (DEPTH, MIX_WIDTH), 0.02),
        'w_out': nrm(ks[22], (DEPTH, MIX_WIDTH, D_MODEL), MIX_WIDTH ** -0.5),
        'final_norm_g': 1.0 + nrm(ks[23], (D_MODEL,), 0.02),
    }


def reference(x, c, ctx, c_ctx, w_mod, b_mod, norm_g, w_ffn_in, w_ffn_out, w_in,
              q_norm_g, k_norm_g, lru_conv_w, lru_conv_b, lru_wa, lru_ba, lru_wx, lru_bx,
              lru_lambda, sc_conv_w, sc_conv_b, grp_norm_g, w_out, final_norm_g):
    bsz = x.shape[0]
    cos, sin = axial_rope_tables(x.shape[1])
    h_ctx = ctx
    for l in range(DEPTH):
        last = l == DEPTH - 1
        mx = (jax.nn.silu(c) @ w_mod[l] + b_mod[l]).reshape(bsz, N_MOD, D_MODEL)
        mc = (jax.nn.silu(c_ctx) @ w_mod[l] + b_mod[l]).reshape(1, N_MOD, D_MODEL)

        x = x + 0.5 * mx[:, 2, None] * swiglu(ada_norm(x, norm_g[l, 0], mx[:, 0], mx[:, 1]),
                                               w_ffn_in[l, 0], w_ffn_out[l, 0])
        h_ctx = h_ctx + 0.5 * mc[:, 2, None] * swiglu(ada_norm(h_ctx, norm_g[l, 0], mc[:, 0], mc[:, 1]),
                                                       w_ffn_in[l, 0], w_ffn_out[l, 0])

        p = {'w_in': w_in[l], 'q_g': q_norm_g[l], 'k_g': k_norm_g[l],
             'lru_conv_w': lru_conv_w[l], 'lru_conv_b': lru_conv_b[l],
             'wa': lru_wa[l], 'ba': lru_ba[l], 'wx': lru_wx[l], 'bx': lru_bx[l], 'lam': lru_lambda[l],
             'sc_conv_w': sc_conv_w[l], 'sc_conv_b': sc_conv_b[l],
             'grp_g': grp_norm_g[l], 'w_out': w_out[l]}
        hx = ada_norm(x, norm_g[l, 1], mx[:, 3], mx[:, 4])
        hc = ada_norm(h_ctx, norm_g[l, 1], mc[:, 3], mc[:, 4])
        out_x, out_c = mixer(hx, hc, p, cos, sin, not last)
        x = x + mx[:, 5, None] * out_x

        x = x + 0.5 * mx[:, 8, None] * swiglu(ada_norm(x, norm_g[l, 2], mx[:, 6], mx[:, 7]),
                                               w_ffn_in[l, 1], w_ffn_out[l, 1])
        if not last:
            h_ctx = h_ctx + mc[:, 5, None] * out_c
            h_ctx = h_ctx + 0.5 * mc[:, 8, None] * swiglu(ada_norm(h_ctx, norm_g[l, 2], mc[:, 6], mc[:, 7]),
                                                           w_ffn_in[l, 1], w_ffn_out[l, 1])
    return rms_norm(x, final_norm_g)
```

```python
import numpy as np
from contextlib import ExitStack
import concourse.bass as bass
import concourse.mybir as mybir
from concourse.bass_utils import run_bass_kernel_spmd
from concourse.ap import AP

F32 = mybir.dt.float32
BF16 = mybir.dt.bfloat16
AF = mybir.ActivationFunctionType
ALU = mybir.AluOpType

D = 1024
DEPTH = 2
CTX = 256
DFF = 2816
EPS = 1e-6
NS = 160
O_BM, O_NG, O_GQK, O_LCW, O_LCB, O_LB, O_LAM, O_SCW, O_SCB, O_GG, O_FG = 0, 72, 96, 100, 108, 110, 118, 122, 128, 130, 138
WIN_COLS = 2944


def fap(ap, dims):
    return AP(ap.tensor, ap.offset, [list(ap.ap[0])] + [list(d) for d in dims])


def rev(ap, n):
    return AP(ap.tensor, ap.offset + (n - 1), [list(ap.ap[0]), [-1, n]])


class Prog:
    ENG = ('sp', 'act', 'dve', 'pool', 'pe')

    def __init__(self, nc, es):
        self.nc = nc
        self.esem = {e: es.enter_context(nc.semaphore("E_" + e)) for e in self.ENG}
        self.ecnt = {e: 0 for e in self.ENG}
        self.dpool = [es.enter_context(nc.semaphore("D%d" % i)) for i in range(72)]
        self.dsem = {}
        self.dcnt = {}
        self.ops = []

    def op(self, eng, fn, r=(), w=(), dma=None):
        self.ops.append(dict(eng=eng, fn=fn, r=tuple(r), w=tuple(w), dma=dma))

    def flush(self):
        ops = self.ops
        self.ops = []
        if not ops:
            return
        last_w = {}
        readers = {}
        need_inc = set()
        for i, o in enumerate(ops):
            o['stream'] = ('d', o['dma']) if o['dma'] else ('e', o['eng'])
            deps = set()
            for r in o['r']:
                if r in last_w:
                    deps.add(last_w[r])
            for w in o['w']:
                if w in last_w:
                    deps.add(last_w[w])
                for j in readers.get(w, {}).values():
                    deps.add(j)
            deps.discard(i)
            if o['eng'] == 'pe' and not o['dma']:
                deps = {j for j in deps if not (ops[j]['eng'] == 'pe' and not ops[j]['dma'])}
            o['deps'] = deps
            need_inc |= deps
            for r in o['r']:
                readers.setdefault(r, {})[o['stream']] = i
            for w in o['w']:
                last_w[w] = i
                readers[w] = {}
        used_d = set()
        for i, o in enumerate(ops):
            if o['dma']:
                k = o['dma']
                if k not in self.dsem:
                    self.dsem[k] = self.dpool[len(self.dsem)]
                    self.dcnt[k] = 0
                self.dcnt[k] += 16
                o['done'] = (self.dsem[k], self.dcnt[k])
                o['inc'] = (self.dsem[k], 16)
                used_d.add(k)
            elif i in need_inc:
                e = o['eng']
                self.ecnt[e] += 1
                o['done'] = (self.esem[e], self.ecnt[e])
                o['inc'] = (self.esem[e], 1)
            else:
                o['inc'] = None
        seen = {e: {} for e in self.ENG}
        for o in ops:
            req = {}
            for j in o['deps']:
                s, v = ops[j]['done']
                if v > req.get(id(s), (s, 0))[1]:
                    req[id(s)] = (s, v)
            ws = []
            sn = seen[o['eng']]
            for sid, (s, v) in req.items():
                if sn.get(sid, 0) >= v:
                    continue
                sn[sid] = v
                ws.append((s, v))
            o['waits'] = ws
        final_d = [(self.dsem[k], self.dcnt[k]) for k in used_d]
        nc = self.nc
        with nc.Block() as block:
            decos = dict(sp=block.sync, act=block.scalar, dve=block.vector, pool=block.gpsimd, pe=block.tensor)
            for en in self.ENG:
                eops = [o for o in ops if o['eng'] == en]
                if not eops and en != 'sp':
                    continue

                def body(e, eops=eops, en=en):
                    for o in eops:
                        ws = o['waits']
                        attach = None
                        if ws and not o['dma']:
                            attach = ws[-1]
                            ws = ws[:-1]
                        for s, v in ws:
                            e.wait_ge(s, v)
                        ins = o['fn'](e)
                        if attach is not None:
                            ins = ins._wait_ge(attach[0], attach[1])
                        if o['inc'] is not None:
                            ins.then_inc(o['inc'][0], o['inc'][1])
                    if en == 'sp':
                        for s, v in final_d:
                            e.wait_ge(s, v)
                decos[en](body)


_UID = [0]
_DBG = {}


def _u(name):
    _UID[0] += 1
    return "%s_%d" % (name, _UID[0])


def build(SEQ, last_ctx_skip=True, debug=False, stop=None, start=0):
    TT = CTX + SEQ
    NT128 = TT // 128
    tiles = [(0, CTX, True)] + [(CTX + 512 * i, 512, False) for i in range(SEQ // 512)]
    nc = bass.Bass("TRN2", target_bir_lowering=False)

    def din(name, shape):
        return nc.dram_tensor(name, list(shape), F32, kind="ExternalInput").ap()

    XT0 = din("xt0", [D, TT])
    CVEC = din("cvec", [128, 8, 2])
    SMALL = din("small", [DEPTH, 128, NS])
    WMOD = din("w_mod", [DEPTH, D, 9 * D])
    WF1 = din("w_ffn_in", [DEPTH, 2, D, 2 * DFF])
    WF2 = din("w_ffn_out", [DEPTH, 2, DFF, D])
    WIN = din("w_inp", [DEPTH, D, WIN_COLS])
    WOUT = din("w_out", [DEPTH, D, D])
    LW = din("lru_w", [DEPTH, 128, 2 * 2 * 2 * 128])
    ROPE = din("rope", [128, 2, TT])
    Y = nc.dram_tensor("y", [D, SEQ], F32, kind="ExternalOutput").ap()
    skind = "ExternalOutput" if debug else "Internal"

    def dscr(name, shape, dt=F32):
        return nc.dram_tensor(name, list(shape), dt, kind=skind).ap()

    XS = dscr("xs", [D, TT])
    QT = dscr("qt", [512, TT], BF16)
    UD = dscr("ud", [256, TT])
    GD = dscr("gd", [256, TT])
    BGD = dscr("bgd", [256, TT])
    CSD = dscr("csd", [256, TT])
    ATTU = dscr("attu", [512, TT])
    DEN = dscr("den", [8, TT])
    HF = dscr("hf", [256, TT])
    LRD = dscr("lrd", [256, TT])
    SCD = dscr("scd", [256, TT])
    UCD = dscr("ucd", [256, TT])

    def fm(ap2d):
        return ap2d.rearrange("(k p) n -> p k n", p=128)

    with ExitStack() as es:
        P = Prog(nc, es)
        sb = lambda name, shape, dt=F32: es.enter_context(nc.sbuf_tensor(_u(name), list(shape), dt))
        SM = sb("SM", [128, NS])
        SV = sb("SV", [128, 8, 2])
        SVB = sb("SVB", [128, 8, 2], BF16)
        MODV = sb("MODV", [128, 72, 2])
        GS = sb("GS", [128, 3, 8, 2])
        GT = sb("GT", [128, 3, 8, 2])
        ONES = sb("ONES", [128, 128], BF16)
        BLK = sb("BLK", [128, 128], BF16)
        C1 = sb("C1", [128, 4])
        C2 = sb("C2", [128, 4])
        LTMP = sb("LTMP", [128, 4])
        EPSV = sb("EPSV", [128, 1])

        def smc(off, l=1):
            return SM[:, off:off + l]

        P.op('dve', lambda e: e.memset(ONES[:], 1.0), w=['ONES'])
        P.op('dve', lambda e: e.memset(BLK[:], 0.0), w=['BLK'])
        P.op('dve', lambda e: e.memset(EPSV[:], EPS), w=['EPSV'])
        P.op('dve', lambda e: e.memset(BLK[0:64, 0:64], 1.0), w=['BLK'])
        P.op('dve', lambda e: e.memset(BLK[64:128, 64:128], 1.0), w=['BLK'])
        P.op('sp', lambda e: e.dma_start(out=SV[:], in_=CVEC[:, :, :]), w=['SV'], dma='SV')
        P.op('act', lambda e: e.activation(SVB[:], SV[:], AF.Silu), r=['SV'], w=['SVB'])
        P.flush()

        def norm_stage(src, c0, n, XL, SQ, RS, TMP, H, PSS, gsi, s, load=True, hres='H'):
            if load:
                P.op('sp', lambda e: e.dma_start(out=XL[:, :, 0:n], in_=fm(src)[:, :, c0:c0 + n]), w=['XL'], dma='XL')
            for k in range(8):
                P.op('act', lambda e, k=k: e.activation(SQ[:, k % 2, 0:n], XL[:, k, 0:n], AF.Square),
                     r=['XL'], w=[('SQ', k % 2)])
                P.op('pe', lambda e, k=k: e.matmul(PSS[:, 0:n], ONES[:], SQ[:, k % 2, 0:n], start=(k == 0), stop=(k == 7)),
                     r=[('SQ', k % 2), 'ONES'], w=['PSS'])
            P.op('act', lambda e: e.activation(RS[:, 0:n], PSS[:, 0:n], AF.Ln, bias=EPSV[:, 0:1], scale=1.0 / D), r=['PSS'], w=['RS'])
            P.op('act', lambda e: e.activation(RS[:, 0:n], RS[:, 0:n], AF.Exp, scale=-0.5), r=['RS'], w=['RS'])
            for k in range(8):
                P.op('dve', lambda e, k=k: e.tensor_tensor(TMP[:, k % 2, 0:n], XL[:, k, 0:n], RS[:, 0:n], ALU.mult),
                     r=['XL', 'RS'], w=[('TMP', k % 2)])
                if gsi is None:
                    continue
                P.op('act', lambda e, k=k: e.activation(H[:, k, 0:n], TMP[:, k % 2, 0:n], AF.Identity,
                                                        bias=MODV[:, gsi * 24 + k, s:s + 1], scale=GS[:, gsi, k, s:s + 1]),
                     r=[('TMP', k % 2), 'MOD'], w=[hres])

        def phase_mod(l):
            with nc.sbuf_tensor(_u("WM"), [128, 2, 8, 1024], BF16) as WM, nc.psum_tensor(_u("PM"), [128, 2, 512], F32) as PM:
                P.op('sp', lambda e: e.dma_start(out=SM[:], in_=SMALL[l]), w=['SM'], dma='SM')
                wv = WMOD[l].rearrange("(k p) c -> p k c", p=128)
                for r in range(9):
                    b = r % 2
                    P.op('pool', lambda e, r=r, b=b: e.dma_start(out=WM[:, b], in_=wv[:, :, r * 1024:(r + 1) * 1024]),
                         w=[('WM', b)], dma='WM%d' % b)
                    for k in range(8):
                        for K in range(8):
                            P.op('pe', lambda e, b=b, k=k, K=K: e.matmul(PM[:, b, 2 * k:2 * k + 2], WM[:, b, K, k * 128:(k + 1) * 128],
                                                                         SVB[:, K, :], start=(K == 0), stop=(K == 7)),
                                 r=[('WM', b), 'SVB'], w=[('PM', b)])
                    P.op('dve', lambda e, r=r, b=b: e.tensor_tensor(
                        MODV[:, r * 8:(r + 1) * 8, :], fap(PM[:, b, 0:16], [[2, 8], [1, 2]]),
                        fap(smc(O_BM + r * 8, 8), [[1, 8], [0, 2]]), ALU.add), r=[('PM', b), 'SM'], w=['MOD'])
                for i in range(3):
                    P.op('dve', lambda e, i=i: e.tensor_scalar(GS[:, i], MODV[:, (3 * i + 1) * 8:(3 * i + 2) * 8, :], 1.0, 1.0, ALU.mult, ALU.add),
                         r=['MOD'], w=['GS'])
                    P.op('dve', lambda e, i=i: e.tensor_tensor(GS[:, i], GS[:, i], fap(smc(O_NG + i * 8, 8), [[1, 8], [0, 2]]), ALU.mult),
                         r=['GS', 'SM'], w=['GS'])
                    P.op('dve', lambda e, i=i: e.tensor_scalar(GT[:, i], MODV[:, (3 * i + 2) * 8:(3 * i + 3) * 8, :],
                                                               (1.0 if i == 1 else 0.5), 0.0, ALU.mult, ALU.add), r=['MOD'], w=['GT'])
                P.op('act', lambda e: e.activation(LTMP[:], smc(O_LAM, 4), AF.Exp, scale=-1.0), r=['SM'], w=['LTMP'])
                P.op('act', lambda e: e.activation(LTMP[:], LTMP[:], AF.Ln, bias=1.0), r=['LTMP'], w=['LTMP'])
                P.op('dve', lambda e: e.tensor_scalar(C1[:], LTMP[:], -8.0, 0.0, ALU.mult, ALU.add), r=['LTMP'], w=['C1'])
                P.op('dve', lambda e: e.tensor_scalar(C2[:], LTMP[:], -16.0, 0.0, ALU.mult, ALU.add), r=['LTMP'], w=['C2'])
                P.flush()

        def phase_ffn(l, f, src, dst, do_ctx):
            gsi = 0 if f == 0 else 2
            tl = [t for t in tiles if do_ctx or not t[2]]
            with ExitStack() as ps:
                t_ = lambda name, shape, dt=F32: ps.enter_context(nc.sbuf_tensor(_u(name), list(shape), dt))
                W1 = t_("W1", [128, 8, 2 * DFF], BF16)
                W2 = t_("W2", [128, 22, D], BF16)
                XL = t_("XL", [128, 8, 512])
                SQ = t_("SQ", [128, 2, 512], BF16)
                RS = t_("RS", [128, 512])
                TMP = t_("TMP", [128, 2, 512])
                H = t_("H", [128, 8, 512], BF16)
                A = t_("A", [128, 22, 512], BF16)
                SG = t_("SG", [128, 2, 512])
                XR = t_("XR", [128, 3, 512])
                pt = lambda name: ps.enter_context(nc.psum_tensor(_u(name), [128, 512], F32))
                PSS = pt("PSS")
                PG = [pt("PG0"), pt("PG1")]
                PU = [pt("PU0"), pt("PU1")]
                PO = [pt("PO0"), pt("PO1")]
                w1v = WF1[l, f].rearrange("(k p) c -> p k c", p=128)
                w2v = WF2[l, f].rearrange("(j p) c -> p j c", p=128)
                for K in range(8):
                    P.op('pool', lambda e, K=K: e.dma_start(out=W1[:, K, :], in_=w1v[:, K, :]), w=[('W1', K)], dma='W1k%d' % K)
                for j0 in range(0, 22, 11):
                    P.op('pool', lambda e, j0=j0: e.dma_start(out=W2[:, j0:j0 + 11, :], in_=w2v[:, j0:j0 + 11, :]), w=[('W2', j0 // 11)], dma='W2k%d' % (j0 // 11))
                xrc = [0]

                def out_chunk(c0, n, s, k):
                    b = xrc[0] % 3
                    xrc[0] += 1
                    P.op('sp', lambda e: e.dma_start(out=XR[:, b, 0:n], in_=src[k * 128:(k + 1) * 128, c0:c0 + n]),
                         w=[('XR', b)], dma='XR%d' % b)
                    for j in range(22):
                        P.op('pe', lambda e, j=j: e.matmul(PO[k % 2][:, 0:n], W2[:, j, k * 128:(k + 1) * 128], A[:, j, 0:n],
                                                           start=(j == 0), stop=(j == 21)), r=[('W2', j // 11), 'A'], w=[('PO', k % 2)])
                    P.op('dve', lambda e: e.scalar_tensor_tensor(XR[:, b, 0:n], PO[k % 2][:, 0:n], GT[:, gsi, k, s:s + 1],
                                                                 XR[:, b, 0:n], ALU.mult, ALU.add),
                         r=[('PO', k % 2), ('XR', b), 'MOD'], w=[('XR', b)])
                    P.op('pool', lambda e: e.dma_start(out=dst[k * 128:(k + 1) * 128, c0:c0 + n], in_=XR[:, b, 0:n]),
                         r=[('XR', b)], dma='XRS%d' % b)

                c0, n, ic = tl[0]
                norm_stage(src, c0, n, XL, SQ, RS, TMP, H, PSS, gsi, 1 if ic else 0)
                def do_tile(c0, n, ic, nxt):
                    s = 1 if ic else 0
                    if nxt is not None:
                        P.op('sp', lambda e, nxt=nxt: e.dma_start(out=XL[:, :, 0:nxt[1]], in_=fm(src)[:, :, nxt[0]:nxt[0] + nxt[1]]),
                             w=['XL'], dma='XL')
                    for j in range(22):
                        for K in range(8):
                            P.op('pe', lambda e, j=j, K=K: e.matmul(PG[j % 2][:, 0:n], W1[:, K, j * 128:(j + 1) * 128], H[:, K, 0:n],
                                                                    start=(K == 0), stop=(K == 7)), r=[('W1', K), 'H'], w=[('PG', j % 2)])
                        for K in range(8):
                            P.op('pe', lambda e, j=j, K=K: e.matmul(PU[j % 2][:, 0:n], W1[:, K, DFF + j * 128:DFF + (j + 1) * 128], H[:, K, 0:n],
                                                                    start=(K == 0), stop=(K == 7)), r=[('W1', K), 'H'], w=[('PU', j % 2)])
                        P.op('act', lambda e, j=j: e.activation(SG[:, j % 2, 0:n], PG[j % 2][:, 0:n], AF.Silu),
                             r=[('PG', j % 2)], w=[('SG', j % 2)])
                        P.op('dve', lambda e, j=j: e.tensor_tensor(A[:, j, 0:n], SG[:, j % 2, 0:n], PU[j % 2][:, 0:n], ALU.mult),
                             r=[('SG', j % 2), ('PU', j % 2)], w=['A'])
                    for k in range(4):
                        out_chunk(c0, n, s, k)
                    if nxt is not None:
                        norm_stage(src, nxt[0], nxt[1], XL, SQ, RS, TMP, H, PSS, gsi, 1 if nxt[2] else 0, load=False)
                    for k in range(4, 8):
                        out_chunk(c0, n, s, k)

                for ti, (c0, n, ic) in enumerate(tl):
                    do_tile(c0, n, ic, tl[ti + 1] if ti + 1 < len(tl) else None)
                P.flush()

        def phase_mix_in(l, KD, VA):
            with ExitStack() as ps:
                t_ = lambda name, shape, dt=F32: ps.enter_context(nc.sbuf_tensor(_u(name), list(shape), dt))
                WI = t_("WI", [128, 8, WIN_COLS], BF16)
                XL = t_("XL", [128, 8, 512])
                SQ = t_("SQ", [128, 2, 512], BF16)
                RS = t_("RS", [128, 512])
                TMP = t_("TMP", [128, 2, 512])
                HH = t_("H", [128, 2, 8, 512], BF16)
                CT = t_("CT", [128, 2, 512])
                QA = t_("QA", [128, 2, 512])
                QB = t_("QB", [128, 2, 512])
                SQH = t_("SQH", [128, 2, 512], BF16)
                RSH = t_("RSH", [128, 2, 512])
                QO = t_("QO", [128, 4, 512], BF16)
                OS = t_("OS", [128, 4, 2, 512])
                CGT = t_("CGT", [128, 2, 512])
                pt = lambda name: ps.enter_context(nc.psum_tensor(_u(name), [128, 512], F32))
                PSS = pt("PSS")
                PA = [pt("PA0"), pt("PA1")]
                PB = [pt("PB0"), pt("PB1")]
                PSH = pt("PSH")
                PX = [pt("PX0"), pt("PX1")]
                wv = WIN[l].rearrange("(k p) c -> p k c", p=128)
                for K in range(8):
                    P.op('pool', lambda e, K=K: e.dma_start(out=WI[:, K, :], in_=wv[:, K, :]), w=[('WI', K)], dma='W1k%d' % K)
                P.op('dve', lambda e: e.memset(VA[:, :, :, 64:65], 1.0), w=['VA1'])
                pxc = [0]
                def tile_body(ti, c0, n, ic, nxt):
                    s = 1 if ic else 0
                    hp = ti % 2
                    H = HH[:, hp]
                    HR = ('H', hp)
                    if nxt is not None:
                        P.op('sp', lambda e: e.dma_start(out=XL[:, :, 0:nxt[1]], in_=fm(XS)[:, :, nxt[0]:nxt[0] + nxt[1]]),
                             w=['XL'], dma='XL')
                    P.op('sp', lambda e, c0=c0, n=n: e.dma_start(out=CT[:, :, 0:n], in_=ROPE[:, :, c0:c0 + n]), w=['CT'], dma='CT')
                    for m in range(6):
                        b = m % 2
                        for (PP, coff, nm) in ((PA, 0, 'PA'), (PB, 768, 'PB')):
                            for K in range(8):
                                P.op('pe', lambda e, PP=PP, coff=coff, K=K, m=m, b=b: e.matmul(
                                    PP[b][:, 0:n], WI[:, K, coff + m * 128:coff + (m + 1) * 128], H[:, K, 0:n],
                                    start=(K == 0), stop=(K == 7)), r=[('WI', K), HR], w=[(nm, b)])
                        P.op('act', lambda e, b=b: e.activation(SQH[:, b, 0:n], PA[b][:, 0:n], AF.Square), r=[('PA', b)], w=[('SQH', b), ('PA', b)])
                        P.op('pe', lambda e, b=b: e.matmul(PSH[:, 0:n], BLK[:], SQH[:, b, 0:n], start=True, stop=True),
                             r=[('SQH', b), 'BLK'], w=['PSH'])
                        P.op('act', lambda e, b=b: e.activation(RSH[:, b, 0:n], PSH[:, 0:n], AF.Ln, bias=EPSV[:, 0:1], scale=1.0 / 64),
                             r=['PSH'], w=[('RSH', b)])
                        P.op('act', lambda e, b=b: e.activation(RSH[:, b, 0:n], RSH[:, b, 0:n], AF.Exp, scale=-0.5), r=[('RSH', b)], w=[('RSH', b)])
                        go = O_GQK + (0 if m < 4 else 2)
                        P.op('dve', lambda e, b=b: e.tensor_tensor(QA[:, b, 0:n], PA[b][:, 0:n], CT[:, 0, 0:n], ALU.mult),
                             r=[('PA', b), 'CT'], w=[('QA', b)])
                        P.op('dve', lambda e, b=b: e.tensor_tensor(QB[:, b, 0:n], PB[b][:, 0:n], CT[:, 1, 0:n], ALU.mult),
                             r=[('PB', b), 'CT'], w=[('QB', b)])
                        P.op('act', lambda e, b=b, go=go: e.activation(QB[:, b, 0:n], QB[:, b, 0:n], AF.Identity, scale=smc(go + 1)),
                             r=[('QB', b), 'SM'], w=[('QB', b)])
                        P.op('dve', lambda e, b=b, go=go: e.scalar_tensor_tensor(QA[:, b, 0:n], QA[:, b, 0:n], smc(go), QB[:, b, 0:n],
                                                                                 ALU.mult, ALU.add),
                             r=[('QA', b), ('QB', b), 'SM'], w=[('QA', b)])
                        if m < 4:
                            P.op('dve', lambda e, b=b, m=m: e.tensor_tensor(QO[:, m, 0:n], QA[:, b, 0:n], RSH[:, b, 0:n], ALU.mult),
                                 r=[('QA', b), ('RSH', b)], w=['QO'])
                        else:
                            P.op('dve', lambda e, b=b, m=m, c0=c0: e.tensor_tensor(KD[:, m - 4, c0:c0 + n], QA[:, b, 0:n], RSH[:, b, 0:n], ALU.mult),
                                 r=[('QA', b), ('RSH', b)], w=['KD'])
                    P.op('pool', lambda e, c0=c0, n=n: e.dma_start(out=fm(QT)[:, :, c0:c0 + n], in_=QO[:, :, 0:n]), r=['QO'], dma='QOS')
                    if nxt is not None:
                        norm_stage(XS, nxt[0], nxt[1], XL, SQ, RS, TMP, HH[:, (ti + 1) % 2], PSS, 1, 1 if nxt[2] else 0,
                                   load=False, hres=('H', (ti + 1) % 2))
                    for sub in range(n // 128):
                        b = pxc[0] % 2
                        pxc[0] += 1
                        for K in range(8):
                            P.op('pe', lambda e, b=b, K=K, sub=sub: e.matmul(PX[b][:, 0:128], H[:, K, sub * 128:(sub + 1) * 128],
                                                                             WI[:, K, 1536:1664], start=(K == 0), stop=(K == 7)),
                                 r=[('WI', K), HR], w=[('PX', b)])
                        P.op('act', lambda e, b=b, sub=sub, c0=c0: e.activation(
                            VA[:, c0 // 128 + sub, :, 0:64], fap(PX[b][:, 0:128], [[64, 2], [1, 64]]), AF.Identity),
                            r=[('PX', b)], w=['VA'])
                    for gi, (dst, col) in enumerate(((UD, 1664), (GD, 1920), (BGD, 2176))):
                        for c in range(2):
                            b = pxc[0] % 2
                            pxc[0] += 1
                            for K in range(8):
                                P.op('pe', lambda e, b=b, K=K, col=col, c=c: e.matmul(
                                    PX[b][:, 0:n], WI[:, K, col + c * 128:col + (c + 1) * 128], H[:, K, 0:n],
                                    start=(K == 0), stop=(K == 7)), r=[('WI', K), HR], w=[('PX', b)])
                            P.op('act', lambda e, b=b, gi=gi, c=c: e.activation(OS[:, gi, c, 0:n], PX[b][:, 0:n], AF.Identity),
                                 r=[('PX', b)], w=[('OS', gi)])
                        P.op('pool', lambda e, dst=dst, gi=gi, c0=c0, n=n: e.dma_start(out=fm(dst)[:, :, c0:c0 + n], in_=OS[:, gi, :, 0:n]),
                             r=[('OS', gi)], dma='OSS%d' % gi)
                    for c in range(2):
                        b = pxc[0] % 2
                        pxc[0] += 1
                        for K in range(8):
                            P.op('pe', lambda e, b=b, K=K, c=c: e.matmul(PX[b][:, 0:n], WI[:, K, 2432 + c * 128:2432 + (c + 1) * 128], H[:, K, 0:n],
                                                                         start=(K == 0), stop=(K == 7)), r=[('WI', K), HR], w=[('PX', b)])
                        P.op('act', lambda e, b=b, c=c: e.activation(CGT[:, c, 0:n], PX[b][:, 0:n], AF.Identity), r=[('PX', b)], w=[('CGT', c)])
                        b2 = pxc[0] % 2
                        pxc[0] += 1
                        for K in range(8):
                            P.op('pe', lambda e, b2=b2, K=K, c=c: e.matmul(PX[b2][:, 0:n], WI[:, K, 2688 + c * 128:2688 + (c + 1) * 128], H[:, K, 0:n],
                                                                           start=(K == 0), stop=(K == 7)), r=[('WI', K), HR], w=[('PX', b2)])
                        P.op('dve', lambda e, b2=b2, c=c: e.tensor_tensor(OS[:, 3, c, 0:n], CGT[:, c, 0:n], PX[b2][:, 0:n], ALU.mult),
                             r=[('PX', b2), ('CGT', c)], w=[('OS', 3)])
                    P.op('pool', lambda e, c0=c0, n=n: e.dma_start(out=fm(CSD)[:, :, c0:c0 + n], in_=OS[:, 3, :, 0:n]),
                         r=[('OS', 3)], dma='OSS3')

                norm_stage(XS, tiles[0][0], tiles[0][1], XL, SQ, RS, TMP, HH[:, 0], PSS, 1, 1 if tiles[0][2] else 0, hres=('H', 0))
                for ti, (c0, n, ic) in enumerate(tiles):
                    tile_body(ti, c0, n, ic, tiles[ti + 1] if ti + 1 < len(tiles) else None)
                P.flush()

        def phase_attn(l, KD, VA, with_ctx):
            with ExitStack() as ps:
                t_ = lambda name, shape, dt=F32: ps.enter_context(nc.sbuf_tensor(_u(name), list(shape), dt))
                QZ = t_("QZ", [128, 2, 8, 512], BF16)
                PT = t_("PT", [128, 3, 2, 512], BF16)
                OSB = t_("OSB", [128, 2, 512])
                PSs = [ps.enter_context(nc.psum_tensor(_u("PS%d" % i), [128, 2, 512], F32)) for i in range(3)]
                PO = [ps.enter_context(nc.psum_tensor(_u("PO%d" % i), [128, 512], F32)) for i in range(2)]
                P.op('dve', lambda e: e.memset(QZ[:], 0.0), w=[('QB', 0), ('QB', 1)])
                sc = [0]
                qtiles = [t for t in tiles if with_ctx or not t[2]]

                def head(h, qb, c0, n, kts):
                    g = h // 4
                    ob = h % 2
                    pairs = [kts[i:i + 2] for i in range(0, len(kts), 2)]
                    slots = []

                    def S(pr):
                        sl = sc[0] % 3
                        sc[0] += 1
                        slots.append(sl)
                        for jj, kt in enumerate(pr):
                            P.op('pe', lambda e, kt=kt, sl=sl, jj=jj: e.matmul(PSs[sl][:, jj, 0:n], KD[:, g, kt * 128:(kt + 1) * 128],
                                                                               QZ[:, qb, h, 0:n], start=True, stop=True),
                                 r=['KD', ('QB', qb)], w=[('PS', sl)])

                    LA = 2
                    for i in range(min(LA, len(pairs))):
                        S(pairs[i])
                    for i, pr in enumerate(pairs):
                        sl = slots[i]
                        P.op('act', lambda e, sl=sl, ng=len(pr): e.activation(PT[:, sl, 0:ng, 0:n], PSs[sl][:, 0:ng, 0:n], AF.Exp, scale=0.125),
                             r=[('PS', sl)], w=[('PT', sl)])
                        if i + LA < len(pairs):
                            S(pairs[i + LA])
                        for jj, kt in enumerate(pr):
                            first = (i == 0 and jj == 0)
                            lastm = (i == len(pairs) - 1 and jj == len(pr) - 1)
                            P.op('pe', lambda e, kt=kt, sl=sl, jj=jj, first=first, lastm=lastm: e.matmul(
                                PO[ob][0:65, 0:n], VA[:, kt, g, 0:65], PT[:, sl, jj, 0:n], start=first, stop=lastm),
                                r=['VA', 'VA1', ('PT', sl)], w=[('PO', ob)])
                    P.op('dve', lambda e: e.tensor_copy(OSB[0:65, ob, 0:n], PO[ob][0:65, 0:n]), r=[('PO', ob)], w=[('OSB', ob)])
                    P.op('dve', lambda e: e.reciprocal(OSB[64:65, ob, 0:n], OSB[64:65, ob, 0:n]), r=[('OSB', ob)], w=[('OSB', ob)])
                    P.op('pool', lambda e: e.dma_start(out=ATTU[h * 64:(h + 1) * 64, c0:c0 + n], in_=OSB[0:64, ob, 0:n]),
                         r=[('OSB', ob)], dma='OSBS%d' % ob)
                    P.op('pool', lambda e: e.dma_start(out=DEN[h:h + 1, c0:c0 + n], in_=OSB[64:65, ob, 0:n]),
                         r=[('OSB', ob)], dma='OSBS%d' % ob)

                for qi, (c0, n, ic) in enumerate(qtiles):
                    qb = qi % 2
                    for par in range(2):
                        P.op('sp', lambda e, par=par, qb=qb, c0=c0, n=n: e.dma_start(
                            out=QZ[par * 64:(par + 1) * 64, qb, par:8:2, 0:n], in_=fm(QT)[par * 64:(par + 1) * 64, :, c0:c0 + n]),
                            w=[('QB', qb)], dma='QB%d' % qb)
                    kts = list(range(CTX // 128)) if ic else list(range(NT128))
                    for h in range(8):
                        head(h, qb, c0, n, kts)
                P.flush()

        def phase_lrusc(l, with_ctx_out):
            with ExitStack() as ps:
                t_ = lambda name, shape, dt=F32: ps.enter_context(nc.sbuf_tensor(_u(name), list(shape), dt))
                LWB = t_("LWB", [128, 8, 128], BF16)
                UH = t_("UH", [128, 2, 2, 516])
                UC = t_("UC", [128, 2, 2, 512])
                UCB = t_("UCB", [128, 2, 2, 512], BF16)
                GR = t_("GR", [128, 2, 2, 512])
                GI = t_("GI", [128, 2, 2, 512])
                AA = t_("AA", [128, 2, 2, 512])
                A2 = t_("A2", [128, 2, 2, 512])
                HO = t_("HO", [128, 2, 2, 512])
                ST = t_("ST", [128, 2])
                CH = t_("CH", [128, 2, 2, 514])
                BGT = t_("BGT", [128, 2, 2, 512])
                SCT = t_("SCT", [128, 2, 2, 512])
                SC2 = t_("SC2", [128, 2, 2, 512])
                HFL = t_("HFL", [128, 2, 2, 512])
                GL = t_("GL", [128, 2, 2, 512])
                G2 = t_("G2", [128, 2, 2, 512])
                pt = lambda name: ps.enter_context(nc.psum_tensor(_u(name), [128, 512], F32))
                PR = [[pt("PR%d%d" % (p, c)) for c in range(2)] for p in range(2)]
                PI = [[pt("PI%d%d" % (p, c)) for c in range(2)] for p in range(2)]
                P.op('pool', lambda e: e.dma_start(out=LWB[:], in_=LW[l].rearrange("p (a b) -> p a b", b=128)), w=['LWB'], dma='W1')

                def seg(ic):
                    return (0, CTX) if ic else (CTX, TT)

                def lru_load(p, c0, n, ic):
                    lo, hi = seg(ic)
                    a = max(c0 - 2, lo)
                    bb = min(c0 + n + 1, hi)
                    if a > c0 - 2:
                        P.op('dve', lambda e: e.memset(UH[:, p, :, 0:2], 0.0), w=[('UH', p)])
                    if bb < c0 + n + 1:
                        P.op('dve', lambda e: e.memset(UH[:, p, :, n + 2:n + 3], 0.0), w=[('UH', p)])
                    P.op('sp', lambda e: e.dma_start(out=UH[:, p, :, a - (c0 - 2):bb - (c0 - 2)], in_=fm(UD)[:, :, a:bb]), w=[('UH', p)], dma='UH%d' % p)

                def lru_compute(d, p, c0, n, ic):
                    if d == 0:
                        for c in range(2):
                            wo = O_LCW + c * 4
                            P.op('act', lambda e, c=c, wo=wo: e.activation(UC[:, p, c, 0:n], UH[:, p, c, 0:n], AF.Identity, bias=smc(O_LCB + c), scale=smc(wo)),
                                 r=[('UH', p), 'SM'], w=[('UC', p)])
                            for j in range(1, 4):
                                P.op('dve', lambda e, c=c, wo=wo, j=j: e.scalar_tensor_tensor(UC[:, p, c, 0:n], UH[:, p, c, j:j + n], smc(wo + j), UC[:, p, c, 0:n],
                                                                                               ALU.mult, ALU.add), r=[('UH', p), 'SM', ('UC', p)], w=[('UC', p)])
                        P.op('pool', lambda e: e.dma_start(out=fm(UCD)[:, :, c0:c0 + n], in_=UC[:, p, :, 0:n]), r=[('UC', p)], dma='UCS%d' % p)
                    P.op('act', lambda e: e.activation(UCB[:, p, :, 0:n], UC[:, p, :, 0:n], AF.Identity), r=[('UC', p)], w=[('UCB', p)])
                    for c in range(2):
                        P.op('pe', lambda e, c=c: e.matmul(PR[p][c][:, 0:n], LWB[:, d * 4 + c, :], UCB[:, p, c, 0:n], start=True, stop=True),
                             r=['LWB', ('UCB', p)], w=[('PR', p, c)])
                        P.op('pe', lambda e, c=c: e.matmul(PI[p][c][:, 0:n], LWB[:, d * 4 + 2 + c, :], UCB[:, p, c, 0:n], start=True, stop=True),
                             r=['LWB', ('UCB', p)], w=[('PI', p, c)])
                        ob = O_LB + d * 4
                        P.op('act', lambda e, c=c, ob=ob: e.activation(GR[:, p, c, 0:n], PR[p][c][:, 0:n], AF.Sigmoid, bias=smc(ob + c)),
                             r=[('PR', p, c), 'SM'], w=[('GR', p, c)])
                        P.op('act', lambda e, c=c, ob=ob: e.activation(GI[:, p, c, 0:n], PI[p][c][:, 0:n], AF.Sigmoid, bias=smc(ob + 2 + c)),
                             r=[('PI', p, c), 'SM'], w=[('GI', p, c)])
                    for c in range(2):
                        P.op('act', lambda e, c=c: e.activation(AA[:, p, c, 0:n], GR[:, p, c, 0:n], AF.Exp, scale=C1[:, d * 2 + c:d * 2 + c + 1]),
                             r=[('GR', p, c), 'C1'], w=[('AA', p, c)])
                        P.op('act', lambda e, c=c: e.activation(A2[:, p, c, 0:n], GR[:, p, c, 0:n], AF.Exp, scale=C2[:, d * 2 + c:d * 2 + c + 1]),
                             r=[('GR', p, c), 'C1'], w=[('A2', p, c)])
                    a2r = [('A2', p, 0), ('A2', p, 1)]
                    gir = [('GI', p, 0), ('GI', p, 1)]
                    P.op('act', lambda e: e.activation(A2[:, p, :, 0:n], A2[:, p, :, 0:n], AF.Sqrt, bias=1.0, scale=-1.0), r=a2r, w=a2r)
                    P.op('dve', lambda e: e.tensor_tensor(GI[:, p, :, 0:n], GI[:, p, :, 0:n], UC[:, p, :, 0:n], ALU.mult), r=gir + [('UC', p)], w=gir)
                    P.op('dve', lambda e: e.tensor_tensor(GI[:, p, :, 0:n], GI[:, p, :, 0:n], A2[:, p, :, 0:n], ALU.mult), r=gir + a2r, w=gir)
                    for c in range(2):
                        if d == 0:
                            P.op('dve', lambda e, c=c: e.tensor_tensor_scan(HO[:, p, c, 0:n], AA[:, p, c, 0:n], GI[:, p, c, 0:n], ST[:, c:c + 1], ALU.mult, ALU.add),
                                 r=[('AA', p, c), ('GI', p, c), 'ST'], w=[('HO', p)])
                            P.op('dve', lambda e, c=c: e.tensor_copy(ST[:, c:c + 1], HO[:, p, c, n - 1:n]), r=[('HO', p)], w=['ST'])
                        else:
                            P.op('dve', lambda e, c=c: e.tensor_tensor_scan(rev(HO[:, p, c, 0:n], n), rev(AA[:, p, c, 0:n], n), rev(GI[:, p, c, 0:n], n),
                                                                            ST[:, c:c + 1], ALU.mult, ALU.add),
                                 r=[('AA', p, c), ('GI', p, c), 'ST'], w=[('HO', p)])
                            P.op('dve', lambda e, c=c: e.tensor_copy(ST[:, c:c + 1], HO[:, p, c, 0:1]), r=[('HO', p)], w=['ST'])

                def sc_load(p, c0, n, ic):
                    lo, hi = seg(ic)
                    a = max(c0 - 1, lo)
                    bb = min(c0 + n + 1, hi)
                    if a > c0 - 1:
                        P.op('dve', lambda e: e.memset(CH[:, p, :, 0:1], 0.0), w=[('CH', p)])
                    if bb < c0 + n + 1:
                        P.op('dve', lambda e: e.memset(CH[:, p, :, n + 1:n + 2], 0.0), w=[('CH', p)])
                    P.op('sp', lambda e: e.dma_start(out=CH[:, p, :, a - (c0 - 1):bb - (c0 - 1)], in_=fm(CSD)[:, :, a:bb]), w=[('CH', p)], dma='CH%d' % p)
                    P.op('sp', lambda e: e.dma_start(out=BGT[:, p, :, 0:n], in_=fm(BGD)[:, :, c0:c0 + n]), w=[('BGT', p)], dma='BGT%d' % p)

                def sc_compute(p, c0, n, ic):
                    for c in range(2):
                        wo = O_SCW + c * 3
                        P.op('act', lambda e, c=c, wo=wo: e.activation(SCT[:, p, c, 0:n], CH[:, p, c, 0:n], AF.Identity, bias=smc(O_SCB + c), scale=smc(wo)),
                             r=[('CH', p), 'SM'], w=[('SCT', p)])
                        for j in range(1, 3):
                            P.op('dve', lambda e, c=c, wo=wo, j=j: e.scalar_tensor_tensor(SCT[:, p, c, 0:n], CH[:, p, c, j:j + n], smc(wo + j), SCT[:, p, c, 0:n],
                                                                                           ALU.mult, ALU.add), r=[('CH', p), 'SM', ('SCT', p)], w=[('SCT', p)])
                    P.op('dve', lambda e: e.tensor_tensor(SCT[:, p, :, 0:n], SCT[:, p, :, 0:n], BGT[:, p, :, 0:n], ALU.mult), r=[('SCT', p), ('BGT', p)], w=[('SCT', p)])
                    P.op('pool', lambda e: e.dma_start(out=fm(SCD)[:, :, c0:c0 + n], in_=SCT[:, p, :, 0:n]), r=[('SCT', p)], dma='SCS%d' % p)

                P.op('dve', lambda e: e.memset(ST[:], 0.0), w=['ST'])
                order = list(tiles)
                lru_load(0, *order[0])
                for ti, (c0, n, ic) in enumerate(order):
                    p = ti % 2
                    do_sc = not (ic and not with_ctx_out)
                    if do_sc:
                        sc_load(p, c0, n, ic)
                    if ti + 1 < len(order):
                        lru_load((ti + 1) % 2, *order[ti + 1])
                    lru_compute(0, p, c0, n, ic)
                    P.op('pool', lambda e, p=p, c0=c0, n=n: e.dma_start(out=fm(HF)[:, :, c0:c0 + n], in_=HO[:, p, :, 0:n]), r=[('HO', p)], dma='HOS%d' % p)
                    if do_sc:
                        sc_compute(p, c0, n, ic)
                P.flush()
                P.op('dve', lambda e: e.memset(ST[:], 0.0), w=['ST'])
                order = [tiles[0]] + tiles[:0:-1]

                def lru_load(p, c0, n, ic):
                    P.op('sp', lambda e: e.dma_start(out=UC[:, p, :, 0:n], in_=fm(UCD)[:, :, c0:c0 + n]), w=[('UC', p)], dma='UH%d' % p)

                lru_load(0, *order[0])

                def comb_load(p, c0, n):
                    P.op('sp', lambda e: e.dma_start(out=HFL[:, p, :, 0:n], in_=fm(HF)[:, :, c0:c0 + n]), w=[('HFL', p)], dma='HFL%d' % p)
                    P.op('sp', lambda e: e.dma_start(out=GL[:, p, :, 0:n], in_=fm(GD)[:, :, c0:c0 + n]), w=[('GL', p)], dma='GL%d' % p)

                def gate(p, n):
                    P.op('act', lambda e: e.activation(G2[:, p, :, 0:n], GL[:, p, :, 0:n], AF.Square, scale=0.21145921592590907), r=[('GL', p)], w=[('G2', p)])
                    P.op('dve', lambda e: e.scalar_tensor_tensor(G2[:, p, :, 0:n], G2[:, p, :, 0:n], 1.0, GL[:, p, :, 0:n], ALU.add, ALU.mult),
                         r=[('G2', p), ('GL', p)], w=[('G2', p)])
                    P.op('act', lambda e: e.activation(G2[:, p, :, 0:n], G2[:, p, :, 0:n], AF.Sigmoid, scale=1.5957691216057308), r=[('G2', p)], w=[('G2', p)])
                    P.op('dve', lambda e: e.tensor_tensor(G2[:, p, :, 0:n], G2[:, p, :, 0:n], GL[:, p, :, 0:n], ALU.mult), r=[('G2', p), ('GL', p)], w=[('G2', p)])

                for ti, (c0, n, ic) in enumerate(order):
                    p = ti % 2
                    do_out = not (ic and not with_ctx_out)
                    if do_out:
                        comb_load(p, c0, n)
                    if ti + 1 < len(order):
                        lru_load((ti + 1) % 2, *order[ti + 1])
                    if do_out:
                        gate(p, n)
                    lru_compute(1, p, c0, n, ic)
                    if do_out:
                        P.op('dve', lambda e, p=p, n=n: e.tensor_tensor(HFL[:, p, :, 0:n], HFL[:, p, :, 0:n], HO[:, p, :, 0:n], ALU.add),
                             r=[('HFL', p), ('HO', p)], w=[('HFL', p)])
                        P.op('dve', lambda e, p=p, n=n: e.tensor_tensor(HFL[:, p, :, 0:n], HFL[:, p, :, 0:n], G2[:, p, :, 0:n], ALU.mult),
                             r=[('HFL', p), ('G2', p)], w=[('HFL', p)])
                        P.op('pool', lambda e, p=p, c0=c0, n=n: e.dma_start(out=fm(LRD)[:, :, c0:c0 + n], in_=HFL[:, p, :, 0:n]), r=[('HFL', p)], dma='HFS%d' % p)
                P.flush()

        def phase_mix_out(l, do_ctx):
            tl = [t for t in tiles if do_ctx or not t[2]]
            with ExitStack() as ps:
                t_ = lambda name, shape, dt=F32: ps.enter_context(nc.sbuf_tensor(_u(name), list(shape), dt))
                WO = t_("WO", [128, 8, D], BF16)
                MX = t_("MX", [128, 2, 8, 512])
                DB = t_("DB", [128, 2, 4, 512])
                SQ8 = t_("SQ8", [128, 8, 512], BF16)
                RG = t_("RG", [128, 3, 512])
                TMP = t_("TMP", [128, 2, 512])
                M = t_("M", [128, 2, 8, 512], BF16)
                XR = t_("XR", [128, 3, 512])
                pt = lambda name: ps.enter_context(nc.psum_tensor(_u(name), [128, 512], F32))
                PG = [pt("PGa"), pt("PGb"), pt("PGc")]
                PO = [pt("PO0"), pt("PO1")]
                wv = WOUT[l].rearrange("(k p) c -> p k c", p=128)
                P.op('pool', lambda e: e.dma_start(out=WO[:], in_=wv), w=['WO'], dma='W1')
                xrc = [0]
                grp = [0, 0, 0, 0, 1, 1, 2, 2]
                gw = [512.0, 256.0, 256.0]

                def loads(ti):
                    c0, n, ic = tl[ti]
                    p = ti % 2
                    P.op('sp', lambda e: e.dma_start(out=MX[:, p, 0:4, 0:n], in_=fm(ATTU)[:, :, c0:c0 + n]), w=[('MX', p, 0)], dma='MX0%d' % p)
                    for h in range(8):
                        P.op('sp', lambda e, h=h: e.dma_start(out=DB[(h % 2) * 64:(h % 2) * 64 + 64, p, h // 2, 0:n],
                                                               in_=DEN[h:h + 1, c0:c0 + n].to_broadcast([64, n])), w=[('DB', p, h)], dma='DB%d' % p)
                    P.op('sp', lambda e: e.dma_start(out=MX[:, p, 4:6, 0:n], in_=fm(LRD)[:, :, c0:c0 + n]), w=[('MX', p, 1)], dma='MX1%d' % p)
                    P.op('sp', lambda e: e.dma_start(out=MX[:, p, 6:8, 0:n], in_=fm(SCD)[:, :, c0:c0 + n]), w=[('MX', p, 2)], dma='MX2%d' % p)

                def frontA1(ti):
                    c0, n, ic = tl[ti]
                    p = ti % 2
                    P.op('dve', lambda e: e.tensor_tensor(MX[:, p, 0:4, 0:n], MX[:, p, 0:4, 0:n], DB[:, p, :, 0:n], ALU.mult),
                         r=[('DB', p, h) for h in range(8)] + [('MX', p, 0)], w=[('MX', p, 0)])
                    for k in range(8):
                        g = grp[k]
                        P.op('act', lambda e, k=k: e.activation(SQ8[:, k, 0:n], MX[:, p, k, 0:n], AF.Square), r=[('MX', p, g)], w=[('SQ8', k)])

                def frontA2(ti):
                    c0, n, ic = tl[ti]
                    for k in range(8):
                        g = grp[k]
                        first = k in (0, 4, 6)
                        lastk = k in (3, 5, 7)
                        P.op('pe', lambda e, k=k, g=g, first=first, lastk=lastk: e.matmul(PG[g][:, 0:n], ONES[:], SQ8[:, k, 0:n], start=first, stop=lastk),
                             r=[('SQ8', k), 'ONES'], w=[('PG', g)])
                    for g in range(3):
                        P.op('act', lambda e, g=g: e.activation(RG[:, g, 0:n], PG[g][:, 0:n], AF.Ln, bias=EPSV[:, 0:1], scale=1.0 / gw[g]), r=[('PG', g)], w=[('RG', g)])
                        P.op('act', lambda e, g=g: e.activation(RG[:, g, 0:n], RG[:, g, 0:n], AF.Exp, scale=-0.5), r=[('RG', g)], w=[('RG', g)])

                def frontC(ti, ks):
                    c0, n, ic = tl[ti]
                    p = ti % 2
                    for k in ks:
                        g = grp[k]
                        P.op('dve', lambda e, k=k, g=g: e.tensor_tensor(TMP[:, k % 2, 0:n], MX[:, p, k, 0:n], RG[:, g, 0:n], ALU.mult),
                             r=[('MX', p, g), ('RG', g)], w=[('TMP', k % 2)])
                        P.op('act', lambda e, k=k: e.activation(M[:, p, k, 0:n], TMP[:, k % 2, 0:n], AF.Identity, scale=smc(O_GG + k)),
                             r=[('TMP', k % 2), 'SM'], w=[('M', p)])

                def back(ti, ks):
                    c0, n, ic = tl[ti]
                    p = ti % 2
                    s = 1 if ic else 0
                    for k in ks:
                        b = xrc[0] % 3
                        xrc[0] += 1
                        P.op('sp', lambda e, k=k, b=b: e.dma_start(out=XR[:, b, 0:n], in_=XS[k * 128:(k + 1) * 128, c0:c0 + n]),
                             w=[('XR', b)], dma='XR%d' % b)
                        for j in range(8):
                            P.op('pe', lambda e, k=k, j=j: e.matmul(PO[k % 2][:, 0:n], WO[:, j, k * 128:(k + 1) * 128], M[:, p, j, 0:n],
                                                                    start=(j == 0), stop=(j == 7)), r=['WO', ('M', p)], w=[('PO', k % 2)])
                        P.op('dve', lambda e, k=k, b=b: e.scalar_tensor_tensor(XR[:, b, 0:n], PO[k % 2][:, 0:n], GT[:, 1, k, s:s + 1],
                                                                               XR[:, b, 0:n], ALU.mult, ALU.add),
                             r=[('PO', k % 2), ('XR', b), 'MOD'], w=[('XR', b)])
                        P.op('pool', lambda e, k=k, b=b: e.dma_start(out=XS[k * 128:(k + 1) * 128, c0:c0 + n], in_=XR[:, b, 0:n]),
                             r=[('XR', b)], dma='XRS%d' % b)

                loads(0)
                if len(tl) > 1:
                    loads(1)
                frontA1(0)
                frontA2(0)
                frontC(0, range(8))
                for ti in range(len(tl)):
                    nx = ti + 1 < len(tl)
                    back(ti, [0])
                    if nx:
                        frontA1(ti + 1)
                    back(ti, [1, 2])
                    if nx:
                        frontA2(ti + 1)
                    back(ti, [3, 4])
                    if nx:
                        frontC(ti + 1, range(0, 4))
                    back(ti, [5, 6])
                    if nx:
                        frontC(ti + 1, range(4, 8))
                    back(ti, [7])
                    if ti + 2 < len(tl):
                        loads(ti + 2)
                P.flush()

        def phase_final():
            with ExitStack() as ps:
                t_ = lambda name, shape, dt=F32: ps.enter_context(nc.sbuf_tensor(_u(name), list(shape), dt))
                XL = t_("XL", [128, 2, 8, 512])
                SQ = t_("SQ", [128, 2, 512], BF16)
                RS = t_("RS", [128, 512])
                TMP = t_("TMP", [128, 2, 512])
                YO = t_("YO", [128, 2, 8, 512])
                PSS = ps.enter_context(nc.psum_tensor(_u("PSS"), [128, 512], F32))
                tl = [t for t in tiles if not t[2]]

                def load(ti):
                    c0, n, ic = tl[ti]
                    p = ti % 2
                    P.op('sp', lambda e: e.dma_start(out=XL[:, p, :, 0:n], in_=fm(XS)[:, :, c0:c0 + n]), w=[('XL', p)], dma='XL%d' % p)

                def comp(ti):
                    c0, n, ic = tl[ti]
                    p = ti % 2
                    for k in range(8):
                        P.op('act', lambda e, k=k: e.activation(SQ[:, k % 2, 0:n], XL[:, p, k, 0:n], AF.Square), r=[('XL', p)], w=[('SQ', k % 2)])
                        P.op('pe', lambda e, k=k: e.matmul(PSS[:, 0:n], ONES[:], SQ[:, k % 2, 0:n], start=(k == 0), stop=(k == 7)),
                             r=[('SQ', k % 2), 'ONES'], w=['PSS'])
                    P.op('act', lambda e: e.activation(RS[:, 0:n], PSS[:, 0:n], AF.Ln, bias=EPSV[:, 0:1], scale=1.0 / D), r=['PSS'], w=['RS'])
                    P.op('act', lambda e: e.activation(RS[:, 0:n], RS[:, 0:n], AF.Exp, scale=-0.5), r=['RS'], w=['RS'])
                    for k in range(8):
                        P.op('dve', lambda e, k=k: e.tensor_tensor(TMP[:, k % 2, 0:n], XL[:, p, k, 0:n], RS[:, 0:n], ALU.mult),
                             r=[('XL', p), 'RS'], w=[('TMP', k % 2)])
                        P.op('act', lambda e, k=k: e.activation(YO[:, p, k, 0:n], TMP[:, k % 2, 0:n], AF.Identity, scale=smc(O_FG + k)),
                             r=[('TMP', k % 2), 'SM'], w=[('YO', p)])
                    P.op('pool', lambda e: e.dma_start(out=fm(Y)[:, :, c0 - CTX:c0 - CTX + n], in_=YO[:, p, :, 0:n]), r=[('YO', p)], dma='YOS%d' % p)

                load(0)
                for ti in range(len(tl)):
                    if ti + 1 < len(tl):
                        load(ti + 1)
                    comp(ti)
                P.flush()

        cnt = [0]

        def go():
            cnt[0] += 1
            return (stop is None or cnt[0] <= stop) and cnt[0] >= start

        for l in range(DEPTH):
            last = (l == DEPTH - 1)
            if go():
                phase_mod(l)
            if go():
                phase_ffn(l, 0, XT0 if l == 0 else XS, XS, True)
            with nc.sbuf_tensor(_u("KD"), [128, 2, TT], BF16) as KD, nc.sbuf_tensor(_u("VA"), [128, NT128, 2, 65], BF16) as VA:
                if go():
                    phase_mix_in(l, KD, VA)
                if go():
                    phase_attn(l, KD, VA, not last)
            if go():
                phase_lrusc(l, not last)
            if go():
                phase_mix_out(l, not last)
            if go():
                phase_ffn(l, 1, XS, XS, not last)
        phase_final()
        _DBG['P'] = P
    return nc


def _partner(d):
    w = (d % 32) // 16
    return d + 16 if w == 0 else d - 16


def prep_shared(inp, SEQ):
    f = np.float32
    TT = CTX + SEQ
    small = np.zeros((DEPTH, 128, NS), f)
    win = np.zeros((DEPTH, D, WIN_COLS), f)
    lw = np.zeros((DEPTH, 128, 8, 128), f)
    dd = np.arange(64)
    part = np.array([_partner(d) for d in dd])
    for l in range(DEPTH):
        small[l, :, O_BM:O_BM + 72] = inp['b_mod'][l].reshape(72, 128).T
        small[l, :, O_NG:O_NG + 24] = inp['norm_g'][l].reshape(24, 128).T
        qg = inp['q_norm_g'][l]
        kg = inp['k_norm_g'][l]
        small[l, :, O_GQK + 0] = np.tile(qg, 2)
        small[l, :, O_GQK + 1] = np.tile(qg[part], 2)
        small[l, :, O_GQK + 2] = np.tile(kg, 2)
        small[l, :, O_GQK + 3] = np.tile(kg[part], 2)
        small[l, :, O_LCW:O_LCW + 8] = inp['lru_conv_w'][l].reshape(4, 2, 128).transpose(2, 1, 0).reshape(128, 8)
        small[l, :, O_LCB:O_LCB + 2] = inp['lru_conv_b'][l].reshape(2, 128).T
        for d in range(2):
            small[l, :, O_LB + d * 4 + 0:O_LB + d * 4 + 2] = inp['lru_ba'][l, d].reshape(2, 128).T
            small[l, :, O_LB + d * 4 + 2:O_LB + d * 4 + 4] = inp['lru_bx'][l, d].reshape(2, 128).T
            small[l, :, O_LAM + d * 2:O_LAM + d * 2 + 2] = inp['lru_lambda'][l, d].reshape(2, 128).T
            for ax, wn in enumerate(('lru_wa', 'lru_wx')):
                for c in range(2):
                    for hb in range(2):
                        lw[l, hb * 64:(hb + 1) * 64, d * 4 + ax * 2 + c, hb * 64:(hb + 1) * 64] = inp[wn][l, d, 2 * c + hb]
        small[l, :, O_SCW:O_SCW + 6] = inp['sc_conv_w'][l].reshape(3, 2, 128).transpose(2, 1, 0).reshape(128, 6)
        small[l, :, O_SCB:O_SCB + 2] = inp['sc_conv_b'][l].reshape(2, 128).T
        small[l, :, O_GG:O_GG + 8] = inp['grp_norm_g'][l].reshape(8, 128).T
        small[l, :, O_FG:O_FG + 8] = inp['final_norm_g'].reshape(8, 128).T
        w = inp['w_in'][l]
        q = w[:, 0:512]
        k = w[:, 512:640]
        hp = (np.arange(512) // 64) * 64 + part[np.arange(512) % 64]
        kp = (np.arange(128) // 64) * 64 + part[np.arange(128) % 64]
        q2 = q[:, hp]
        k2 = k[:, kp]
        win[l, :, 0:512] = q
        win[l, :, 512:640] = np.concatenate([k[:, 0:64], k[:, 0:64]], 1)
        win[l, :, 640:768] = np.concatenate([k[:, 64:128], k[:, 64:128]], 1)
        win[l, :, 768:1280] = q2
        win[l, :, 1280:1408] = np.concatenate([k2[:, 0:64], k2[:, 0:64]], 1)
        win[l, :, 1408:1536] = np.concatenate([k2[:, 64:128], k2[:, 64:128]], 1)
        win[l, :, 1536:1664] = w[:, 640:768]
        win[l, :, 1664:2944] = w[:, 768:2048]
    rows = SEQ // 64
    row_ids = np.repeat(np.arange(rows), 64).astype(f)
    col_ids = np.tile(np.arange(64), rows).astype(f)
    inv = (np.float32(10000.0) ** (-np.arange(16, dtype=f) / np.float32(16))).astype(f)
    rope = np.zeros((128, 2, TT), f)
    rope[:, 0, :CTX] = 1.0
    for p in range(128):
        d = p % 64
        half = d // 32
        j = d % 16
        wsel = (d % 32) // 16
        ang = (row_ids if half == 0 else col_ids) * inv[j]
        rope[p, 0, CTX:] = np.cos(ang)
        sn = np.sin(ang)
        rope[p, 1, CTX:] = -sn if wsel == 0 else sn
    return dict(small=small, w_inp=win, lru_w=lw.reshape(DEPTH, 128, 8 * 128), rope=rope,
                w_mod=np.ascontiguousarray(inp['w_mod'], f), w_ffn_in=np.ascontiguousarray(inp['w_ffn_in'], f),
                w_ffn_out=np.ascontiguousarray(inp['w_ffn_out'], f), w_out=np.ascontiguousarray(inp['w_out'], f))


def run(inp, debug=False, n_cores=None, stop=None, start=0):
    x = np.asarray(inp['x'])
    B, SEQ, _ = x.shape
    inp = {k: np.asarray(v, np.float32) for k, v in inp.items()}
    shared = prep_shared(inp, SEQ)
    nc = build(SEQ, debug=debug, stop=stop, start=start)
    in_maps = []
    for b in range(B):
        m = dict(shared)
        m['xt0'] = np.ascontiguousarray(np.concatenate([inp['ctx'][b].T, inp['x'][b].T], axis=1))
        cv = np.stack([inp['c'][b].reshape(8, 128).T, inp['c_ctx'].reshape(8, 128).T], axis=-1)
        m['cvec'] = np.ascontiguousarray(cv, np.float32)
        in_maps.append(m)
    res = run_bass_kernel_spmd(nc, in_maps, core_ids=list(range(B)))
    out = np.stack([np.ascontiguousarray(r['y'].T) for r in res.results], axis=0).astype(np.float32)
    if debug:
        return out, res.results
    return out


def kernel(**inputs):
    return run(inputs)
```

```python
import numpy as np
from contextlib import ExitStack
import concourse.bass as bass
import concourse.mybir as mybir
from concourse.bass_utils import run_bass_kernel_spmd
from concourse.ap import AP

F32 = mybir.dt.float32
BF16 = mybir.dt.bfloat16
AF = mybir.ActivationFunctionType
ALU = mybir.AluOpType

D = 1024
DEPTH = 2
CTX = 256
DFF = 2816
EPS = 1e-6
NS = 160
O_BM, O_NG, O_GQK, O_LCW, O_LCB, O_LB, O_LAM, O_SCW, O_SCB, O_GG, O_FG = 0, 72, 96, 100, 108, 110, 118, 122, 128, 130, 138
WIN_COLS = 2944


def fap(ap, dims):
    return AP(ap.tensor, ap.offset, [list(ap.ap[0])] + [list(d) for d in dims])


def rev(ap, n):
    return AP(ap.tensor, ap.offset + (n - 1), [list(ap.ap[0]), [-1, n]])


class Prog:
    ENG = ('sp', 'act', 'dve', 'pool', 'pe')

    def __init__(self, nc, es):
        self.nc = nc
        self.esem = {e: es.enter_context(nc.semaphore("E_" + e)) for e in self.ENG}
        self.ecnt = {e: 0 for e in self.ENG}
        self.dpool = [es.enter_context(nc.semaphore("D%d" % i)) for i in range(72)]
        self.dsem = {}
        self.dcnt = {}
        self.ops = []

    def op(self, eng, fn, r=(), w=(), dma=None):
        self.ops.append(dict(eng=eng, fn=fn, r=tuple(r), w=tuple(w), dma=dma))

    def flush(self):
        ops = self.ops
        self.ops = []
        if not ops:
            return
        last_w = {}
        readers = {}
        need_inc = set()
        for i, o in enumerate(ops):
            o['stream'] = ('d', o['dma']) if o['dma'] else ('e', o['eng'])
            deps = set()
            for r in o['r']:
                if r in last_w:
                    deps.add(last_w[r])
            for w in o['w']:
                if w in last_w:
                    deps.add(last_w[w])
                for j in readers.get(w, {}).values():
                    deps.add(j)
            deps.discard(i)
            if o['eng'] == 'pe' and not o['dma']:
                deps = {j for j in deps if not (ops[j]['eng'] == 'pe' and not ops[j]['dma'])}
            o['deps'] = deps
            need_inc |= deps
            for r in o['r']:
                readers.setdefault(r, {})[o['stream']] = i
            for w in o['w']:
                last_w[w] = i
                readers[w] = {}
        used_d = set()
        for i, o in enumerate(ops):
            if o['dma']:
                k = o['dma']
                if k not in self.dsem:
                    self.dsem[k] = self.dpool[len(self.dsem)]
                    self.dcnt[k] = 0
                self.dcnt[k] += 16
                o['done'] = (self.dsem[k], self.dcnt[k])
                o['inc'] = (self.dsem[k], 16)
                used_d.add(k)
            elif i in need_inc:
                e = o['eng']
                self.ecnt[e] += 1
                o['done'] = (self.esem[e], self.ecnt[e])
                o['inc'] = (self.esem[e], 1)
            else:
                o['inc'] = None
        seen = {e: {} for e in self.ENG}
        for o in ops:
            req = {}
            for j in o['deps']:
                s, v = ops[j]['done']
                if v > req.get(id(s), (s, 0))[1]:
                    req[id(s)] = (s, v)
            ws = []
            sn = seen[o['eng']]
            for sid, (s, v) in req.items():
                if sn.get(sid, 0) >= v:
                    continue
                sn[sid] = v
                ws.append((s, v))
            o['waits'] = ws
        final_d = [(self.dsem[k], self.dcnt[k]) for k in used_d]
        nc = self.nc
        with nc.Block() as block:
            decos = dict(sp=block.sync, act=block.scalar, dve=block.vector, pool=block.gpsimd, pe=block.tensor)
            for en in self.ENG:
                eops = [o for o in ops if o['eng'] == en]
                if not eops and en != 'sp':
                    continue

                def body(e, eops=eops, en=en):
                    for o in eops:
                        ws = o['waits']
                        attach = None
                        if ws and not o['dma']:
                            attach = ws[-1]
                            ws = ws[:-1]
                        for s, v in ws:
                            e.wait_ge(s, v)
                        ins = o['fn'](e)
                        if attach is not None:
                            ins = ins._wait_ge(attach[0], attach[1])
                        if o['inc'] is not None:
                            ins.then_inc(o['inc'][0], o['inc'][1])
                    if en == 'sp':
                        for s, v in final_d:
                            e.wait_ge(s, v)
                decos[en](body)


_UID = [0]
_DBG = {}


def _u(name):
    _UID[0] += 1
    return "%s_%d" % (name, _UID[0])


def build(SEQ, last_ctx_skip=True, debug=False, stop=None, start=0):
    TT = CTX + SEQ
    NT128 = TT // 128
    tiles = [(0, CTX, True)] + [(CTX + 512 * i, 512, False) for i in range(SEQ // 512)]
    nc = bass.Bass("TRN2", target_bir_lowering=False)

    def din(name, shape):
        return nc.dram_tensor(name, list(shape), F32, kind="ExternalInput").ap()

    XT0 = din("xt0", [D, TT])
    CVEC = din("cvec", [128, 8, 2])
    SMALL = din("small", [DEPTH, 128, NS])
    WMOD = din("w_mod", [DEPTH, D, 9 * D])
    WF1 = din("w_ffn_in", [DEPTH, 2, D, 2 * DFF])
    WF2 = din("w_ffn_out", [DEPTH, 2, DFF, D])
    WIN = din("w_inp", [DEPTH, D, WIN_COLS])
    WOUT = din("w_out", [DEPTH, D, D])
    LW = din("lru_w", [DEPTH, 128, 2 * 2 * 2 * 128])
    ROPE = din("rope", [128, 2, TT])
    Y = nc.dram_tensor("y", [D, SEQ], F32, kind="ExternalOutput").ap()
    skind = "ExternalOutput" if debug else "Internal"

    def dscr(name, shape, dt=F32):
        return nc.dram_tensor(name, list(shape), dt, kind=skind).ap()

    XS = dscr("xs", [D, TT])
    QT = dscr("qt", [512, TT], BF16)
    UD = dscr("ud", [256, TT])
    GD = dscr("gd", [256, TT])
    BGD = dscr("bgd", [256, TT])
    CSD = dscr("csd", [256, TT])
    ATTU = dscr("attu", [512, TT])
    DEN = dscr("den", [8, TT])
    HF = dscr("hf", [256, TT])
    LRD = dscr("lrd", [256, TT])
    SCD = dscr("scd", [256, TT])
    UCD = dscr("ucd", [256, TT])

    def fm(ap2d):
        return ap2d.rearrange("(k p) n -> p k n", p=128)

    with ExitStack() as es:
        P = Prog(nc, es)
        sb = lambda name, shape, dt=F32: es.enter_context(nc.sbuf_tensor(_u(name), list(shape), dt))
        SM = sb("SM", [128, NS])
        SV = sb("SV", [128, 8, 2])
        SVB = sb("SVB", [128, 8, 2], BF16)
        MODV = sb("MODV", [128, 72, 2])
        GS = sb("GS", [128, 3, 8, 2])
        GT = sb("GT", [128, 3, 8, 2])
        ONES = sb("ONES", [128, 128], BF16)
        BLK = sb("BLK", [128, 128], BF16)
        C1 = sb("C1", [128, 4])
        C2 = sb("C2", [128, 4])
        LTMP = sb("LTMP", [128, 4])
        EPSV = sb("EPSV", [128, 1])

        def smc(off, l=1):
            return SM[:, off:off + l]

        P.op('dve', lambda e: e.memset(ONES[:], 1.0), w=['ONES'])
        P.op('dve', lambda e: e.memset(BLK[:], 0.0), w=['BLK'])
        P.op('dve', lambda e: e.memset(EPSV[:], EPS), w=['EPSV'])
        P.op('dve', lambda e: e.memset(BLK[0:64, 0:64], 1.0), w=['BLK'])
        P.op('dve', lambda e: e.memset(BLK[64:128, 64:128], 1.0), w=['BLK'])
        P.op('sp', lambda e: e.dma_start(out=SV[:], in_=CVEC[:, :, :]), w=['SV'], dma='SV')
        P.op('act', lambda e: e.activation(SVB[:], SV[:], AF.Silu), r=['SV'], w=['SVB'])
        P.flush()

        def norm_stage(src, c0, n, XL, SQ, RS, TMP, H, PSS, gsi, s, load=True, hres='H'):
            if load:
                P.op('sp', lambda e: e.dma_start(out=XL[:, :, 0:n], in_=fm(src)[:, :, c0:c0 + n]), w=['XL'], dma='XL')
            for k in range(8):
                P.op('act', lambda e, k=k: e.activation(SQ[:, k % 2, 0:n], XL[:, k, 0:n], AF.Square),
                     r=['XL'], w=[('SQ', k % 2)])
                P.op('pe', lambda e, k=k: e.matmul(PSS[:, 0:n], ONES[:], SQ[:, k % 2, 0:n], start=(k == 0), stop=(k == 7)),
                     r=[('SQ', k % 2), 'ONES'], w=['PSS'])
            P.op('act', lambda e: e.activation(RS[:, 0:n], PSS[:, 0:n], AF.Ln, bias=EPSV[:, 0:1], scale=1.0 / D), r=['PSS'], w=['RS'])
            P.op('act', lambda e: e.activation(RS[:, 0:n], RS[:, 0:n], AF.Exp, scale=-0.5), r=['RS'], w=['RS'])
            for k in range(8):
                P.op('dve', lambda e, k=k: e.tensor_tensor(TMP[:, k % 2, 0:n], XL[:, k, 0:n], RS[:, 0:n], ALU.mult),
                     r=['XL', 'RS'], w=[('TMP', k % 2)])
                if gsi is None:
                    continue
                P.op('act', lambda e, k=k: e.activation(H[:, k, 0:n], TMP[:, k % 2, 0:n], AF.Identity,
                                                        bias=MODV[:, gsi * 24 + k, s:s + 1], scale=GS[:, gsi, k, s:s + 1]),
                     r=[('TMP', k % 2), 'MOD'], w=[hres])

        def phase_mod(l):
            with nc.sbuf_tensor(_u("WM"), [128, 2, 8, 1024], BF16) as WM, nc.psum_tensor(_u("PM"), [128, 2, 512], F32) as PM:
                P.op('sp', lambda e: e.dma_start(out=SM[:], in_=SMALL[l]), w=['SM'], dma='SM')
                wv = WMOD[l].rearrange("(k p) c -> p k c", p=128)
                for r in range(9):
                    b = r % 2
                    P.op('pool', lambda e, r=r, b=b: e.dma_start(out=WM[:, b], in_=wv[:, :, r * 1024:(r + 1) * 1024]),
                         w=[('WM', b)], dma='WM%d' % b)
                    for k in range(8):
                        for K in range(8):
                            P.op('pe', lambda e, b=b, k=k, K=K: e.matmul(PM[:, b, 2 * k:2 * k + 2], WM[:, b, K, k * 128:(k + 1) * 128],
                                                                         SVB[:, K, :], start=(K == 0), stop=(K == 7)),
                                 r=[('WM', b), 'SVB'], w=[('PM', b)])
                    P.op('dve', lambda e, r=r, b=b: e.tensor_tensor(
                        MODV[:, r * 8:(r + 1) * 8, :], fap(PM[:, b, 0:16], [[2, 8], [1, 2]]),
                        fap(smc(O_BM + r * 8, 8), [[1, 8], [0, 2]]), ALU.add), r=[('PM', b), 'SM'], w=['MOD'])
                for i in range(3):
                    P.op('dve', lambda e, i=i: e.tensor_scalar(GS[:, i], MODV[:, (3 * i + 1) * 8:(3 * i + 2) * 8, :], 1.0, 1.0, ALU.mult, ALU.add),
                         r=['MOD'], w=['GS'])
                    P.op('dve', lambda e, i=i: e.tensor_tensor(GS[:, i], GS[:, i], fap(smc(O_NG + i * 8, 8), [[1, 8], [0, 2]]), ALU.mult),
                         r=['GS', 'SM'], w=['GS'])
                    P.op('dve', lambda e, i=i: e.tensor_scalar(GT[:, i], MODV[:, (3 * i + 2) * 8:(3 * i + 3) * 8, :],
                                                               (1.0 if i == 1 else 0.5), 0.0, ALU.mult, ALU.add), r=['MOD'], w=['GT'])
                P.op('act', lambda e: e.activation(LTMP[:], smc(O_LAM, 4), AF.Exp, scale=-1.0), r=['SM'], w=['LTMP'])
                P.op('act', lambda e: e.activation(LTMP[:], LTMP[:], AF.Ln, bias=1.0), r=['LTMP'], w=['LTMP'])
                P.op('dve', lambda e: e.tensor_scalar(C1[:], LTMP[:], -8.0, 0.0, ALU.mult, ALU.add), r=['LTMP'], w=['C1'])
                P.op('dve', lambda e: e.tensor_scalar(C2[:], LTMP[:], -16.0, 0.0, ALU.mult, ALU.add), r=['LTMP'], w=['C2'])
                P.flush()

        def phase_ffn(l, f, src, dst, do_ctx):
            gsi = 0 if f == 0 else 2
            tl = [t for t in tiles if do_ctx or not t[2]]
            with ExitStack() as ps:
                t_ = lambda name, shape, dt=F32: ps.enter_context(nc.sbuf_tensor(_u(name), list(shape), dt))
                W1 = t_("W1", [128, 8, 2 * DFF], BF16)
                W2 = t_("W2", [128, 22, D], BF16)
                XL = t_("XL", [128, 8, 512])
                SQ = t_("SQ", [128, 2, 512], BF16)
                RS = t_("RS", [128, 512])
                TMP = t_("TMP", [128, 2, 512])
                H = t_("H", [128, 8, 512], BF16)
                A = t_("A", [128, 22, 512], BF16)
                SG = t_("SG", [128, 2, 512])
                XR = t_("XR", [128, 3, 512])
                pt = lambda name: ps.enter_context(nc.psum_tensor(_u(name), [128, 512], F32))
                PSS = pt("PSS")
                PG = [pt("PG0"), pt("PG1")]
                PU = [pt("PU0"), pt("PU1")]
                PO = [pt("PO0"), pt("PO1")]
                w1v = WF1[l, f].rearrange("(k p) c -> p k c", p=128)
                w2v = WF2[l, f].rearrange("(j p) c -> p j c", p=128)
                for K in range(8):
                    P.op('pool', lambda e, K=K: e.dma_start(out=W1[:, K, :], in_=w1v[:, K, :]), w=[('W1', K)], dma='W1k%d' % K)
                for j0 in range(0, 22, 11):
                    P.op('pool', lambda e, j0=j0: e.dma_start(out=W2[:, j0:j0 + 11, :], in_=w2v[:, j0:j0 + 11, :]), w=[('W2', j0 // 11)], dma='W2k%d' % (j0 // 11))
                xrc = [0]

                def out_chunk(c0, n, s, k):
                    b = xrc[0] % 3
                    xrc[0] += 1
                    P.op('sp', lambda e: e.dma_start(out=XR[:, b, 0:n], in_=src[k * 128:(k + 1) * 128, c0:c0 + n]),
                         w=[('XR', b)], dma='XR%d' % b)
                    for j in range(22):
                        P.op('pe', lambda e, j=j: e.matmul(PO[k % 2][:, 0:n], W2[:, j, k * 128:(k + 1) * 128], A[:, j, 0:n],
                                                           start=(j == 0), stop=(j == 21)), r=[('W2', j // 11), 'A'], w=[('PO', k % 2)])
                    P.op('dve', lambda e: e.scalar_tensor_tensor(XR[:, b, 0:n], PO[k % 2][:, 0:n], GT[:, gsi, k, s:s + 1],
                                                                 XR[:, b, 0:n], ALU.mult, ALU.add),
                         r=[('PO', k % 2), ('XR', b), 'MOD'], w=[('XR', b)])
                    P.op('pool', lambda e: e.dma_start(out=dst[k * 128:(k + 1) * 128, c0:c0 + n], in_=XR[:, b, 0:n]),
                         r=[('XR', b)], dma='XRS%d' % b)

                c0, n, ic = tl[0]
                norm_stage(src, c0, n, XL, SQ, RS, TMP, H, PSS, gsi, 1 if ic else 0)
                def do_tile(c0, n, ic, nxt):
                    s = 1 if ic else 0
                    if nxt is not None:
                        P.op('sp', lambda e, nxt=nxt: e.dma_start(out=XL[:, :, 0:nxt[1]], in_=fm(src)[:, :, nxt[0]:nxt[0] + nxt[1]]),
                             w=['XL'], dma='XL')
                    for j in range(22):
                        for K in range(8):
                            P.op('pe', lambda e, j=j, K=K: e.matmul(PG[j % 2][:, 0:n], W1[:, K, j * 128:(j + 1) * 128], H[:, K, 0:n],
                                                                    start=(K == 0), stop=(K == 7)), r=[('W1', K), 'H'], w=[('PG', j % 2)])
                        for K in range(8):
                            P.op('pe', lambda e, j=j, K=K: e.matmul(PU[j % 2][:, 0:n], W1[:, K, DFF + j * 128:DFF + (j + 1) * 128], H[:, K, 0:n],
                                                                    start=(K == 0), stop=(K == 7)), r=[('W1', K), 'H'], w=[('PU', j % 2)])
                        P.op('act', lambda e, j=j: e.activation(SG[:, j % 2, 0:n], PG[j % 2][:, 0:n], AF.Silu),
                             r=[('PG', j % 2)], w=[('SG', j % 2)])
                        P.op('dve', lambda e, j=j: e.tensor_tensor(A[:, j, 0:n], SG[:, j % 2, 0:n], PU[j % 2][:, 0:n], ALU.mult),
                             r=[('SG', j % 2), ('PU', j % 2)], w=['A'])
                    for k in range(4):
                        out_chunk(c0, n, s, k)
                    if nxt is not None:
                        norm_stage(src, nxt[0], nxt[1], XL, SQ, RS, TMP, H, PSS, gsi, 1 if nxt[2] else 0, load=False)
                    for k in range(4, 8):
                        out_chunk(c0, n, s, k)

                for ti, (c0, n, ic) in enumerate(tl):
                    do_tile(c0, n, ic, tl[ti + 1] if ti + 1 < len(tl) else None)
                P.flush()

        def phase_mix_in(l, KD, VA):
            with ExitStack() as ps:
                t_ = lambda name, shape, dt=F32: ps.enter_context(nc.sbuf_tensor(_u(name), list(shape), dt))
                WI = t_("WI", [128, 8, WIN_COLS], BF16)
                XL = t_("XL", [128, 8, 512])
                SQ = t_("SQ", [128, 2, 512], BF16)
                RS = t_("RS", [128, 512])
                TMP = t_("TMP", [128, 2, 512])
                HH = t_("H", [128, 2, 8, 512], BF16)
                CT = t_("CT", [128, 2, 512])
                QA = t_("QA", [128, 2, 512])
                QB = t_("QB", [128, 2, 512])
                SQH = t_("SQH", [128, 2, 512], BF16)
                RSH = t_("RSH", [128, 2, 512])
                QO = t_("QO", [128, 4, 512], BF16)
                OS = t_("OS", [128, 4, 2, 512])
                CGT = t_("CGT", [128, 2, 512])
                pt = lambda name: ps.enter_context(nc.psum_tensor(_u(name), [128, 512], F32))
                PSS = pt("PSS")
                PA = [pt("PA0"), pt("PA1")]
                PB = [pt("PB0"), pt("PB1")]
                PSH = pt("PSH")
                PX = [pt("PX0"), pt("PX1")]
                wv = WIN[l].rearrange("(k p) c -> p k c", p=128)
                for K in range(8):
                    P.op('pool', lambda e, K=K: e.dma_start(out=WI[:, K, :], in_=wv[:, K, :]), w=[('WI', K)], dma='W1k%d' % K)
                P.op('dve', lambda e: e.memset(VA[:, :, :, 64:65], 1.0), w=['VA1'])
                pxc = [0]
                def tile_body(ti, c0, n, ic, nxt):
                    s = 1 if ic else 0
                    hp = ti % 2
                    H = HH[:, hp]
                    HR = ('H', hp)
                    if nxt is not None:
                        P.op('sp', lambda e: e.dma_start(out=XL[:, :, 0:nxt[1]], in_=fm(XS)[:, :, nxt[0]:nxt[0] + nxt[1]]),
                             w=['XL'], dma='XL')
                    P.op('sp', lambda e, c0=c0, n=n: e.dma_start(out=CT[:, :, 0:n], in_=ROPE[:, :, c0:c0 + n]), w=['CT'], dma='CT')
                    for m in range(6):
                        b = m % 2
                        for (PP, coff, nm) in ((PA, 0, 'PA'), (PB, 768, 'PB')):
                            for K in range(8):
                                P.op('pe', lambda e, PP=PP, coff=coff, K=K, m=m, b=b: e.matmul(
                                    PP[b][:, 0:n], WI[:, K, coff + m * 128:coff + (m + 1) * 128], H[:, K, 0:n],
                                    start=(K == 0), stop=(K == 7)), r=[('WI', K), HR], w=[(nm, b)])
                        P.op('act', lambda e, b=b: e.activation(SQH[:, b, 0:n], PA[b][:, 0:n], AF.Square), r=[('PA', b)], w=[('SQH', b), ('PA', b)])
                        P.op('pe', lambda e, b=b: e.matmul(PSH[:, 0:n], BLK[:], SQH[:, b, 0:n], start=True, stop=True),
                             r=[('SQH', b), 'BLK'], w=['PSH'])
                        P.op('act', lambda e, b=b: e.activation(RSH[:, b, 0:n], PSH[:, 0:n], AF.Ln, bias=EPSV[:, 0:1], scale=1.0 / 64),
                             r=['PSH'], w=[('RSH', b)])
                        P.op('act', lambda e, b=b: e.activation(RSH[:, b, 0:n], RSH[:, b, 0:n], AF.Exp, scale=-0.5), r=[('RSH', b)], w=[('RSH', b)])
                        go = O_GQK + (0 if m < 4 else 2)
                        P.op('dve', lambda e, b=b: e.tensor_tensor(QA[:, b, 0:n], PA[b][:, 0:n], CT[:, 0, 0:n], ALU.mult),
                             r=[('PA', b), 'CT'], w=[('QA', b)])
                        P.op('dve', lambda e, b=b: e.tensor_tensor(QB[:, b, 0:n], PB[b][:, 0:n], CT[:, 1, 0:n], ALU.mult),
                             r=[('PB', b), 'CT'], w=[('QB', b)])
                        P.op('act', lambda e, b=b, go=go: e.activation(QB[:, b, 0:n], QB[:, b, 0:n], AF.Identity, scale=smc(go + 1)),
                             r=[('QB', b), 'SM'], w=[('QB', b)])
                        P.op('dve', lambda e, b=b, go=go: e.scalar_tensor_tensor(QA[:, b, 0:n], QA[:, b, 0:n], smc(go), QB[:, b, 0:n],
                                                                                 ALU.mult, ALU.add),
                             r=[('QA', b), ('QB', b), 'SM'], w=[('QA', b)])
                        if m < 4:
                            P.op('dve', lambda e, b=b, m=m: e.tensor_tensor(QO[:, m, 0:n], QA[:, b, 0:n], RSH[:, b, 0:n], ALU.mult),
                                 r=[('QA', b), ('RSH', b)], w=['QO'])
                        else:
                            P.op('dve', lambda e, b=b, m=m, c0=c0: e.tensor_tensor(KD[:, m - 4, c0:c0 + n], QA[:, b, 0:n], RSH[:, b, 0:n], ALU.mult),
                                 r=[('QA', b), ('RSH', b)], w=['KD'])
                    P.op('pool', lambda e, c0=c0, n=n: e.dma_start(out=fm(QT)[:, :, c0:c0 + n], in_=QO[:, :, 0:n]), r=['QO'], dma='QOS')
                    if nxt is not None:
                        norm_stage(XS, nxt[0], nxt[1], XL, SQ, RS, TMP, HH[:, (ti + 1) % 2], PSS, 1, 1 if nxt[2] else 0,
                                   load=False, hres=('H', (ti + 1) % 2))
                    for sub in range(n // 128):
                        b = pxc[0] % 2
                        pxc[0] += 1
                        for K in range(8):
                            P.op('pe', lambda e, b=b, K=K, sub=sub: e.matmul(PX[b][:, 0:128], H[:, K, sub * 128:(sub + 1) * 128],
                                                                             WI[:, K, 1536:1664], start=(K == 0), stop=(K == 7)),
                                 r=[('WI', K), HR], w=[('PX', b)])
                        P.op('act', lambda e, b=b, sub=sub, c0=c0: e.activation(
                            VA[:, c0 // 128 + sub, :, 0:64], fap(PX[b][:, 0:128], [[64, 2], [1, 64]]), AF.Identity),
                            r=[('PX', b)], w=['VA'])
                    for gi, (dst, col) in enumerate(((UD, 1664), (GD, 1920), (BGD, 2176))):
                        for c in range(2):
                            b = pxc[0] % 2
                            pxc[0] += 1
                            for K in range(8):
                                P.op('pe', lambda e, b=b, K=K, col=col, c=c: e.matmul(
                                    PX[b][:, 0:n], WI[:, K, col + c * 128:col + (c + 1) * 128], H[:, K, 0:n],
                                    start=(K == 0), stop=(K == 7)), r=[('WI', K), HR], w=[('PX', b)])
                            P.op('act', lambda e, b=b, gi=gi, c=c: e.activation(OS[:, gi, c, 0:n], PX[b][:, 0:n], AF.Identity),
                                 r=[('PX', b)], w=[('OS', gi)])
                        P.op('pool', lambda e, dst=dst, gi=gi, c0=c0, n=n: e.dma_start(out=fm(dst)[:, :, c0:c0 + n], in_=OS[:, gi, :, 0:n]),
                             r=[('OS', gi)], dma='OSS%d' % gi)
                    for c in range(2):
                        b = pxc[0] % 2
                        pxc[0] += 1
                        for K in range(8):
                            P.op('pe', lambda e, b=b, K=K, c=c: e.matmul(PX[b][:, 0:n], WI[:, K, 2432 + c * 128:2432 + (c + 1) * 128], H[:, K, 0:n],
                                                                         start=(K == 0), stop=(K == 7)), r=[('WI', K), HR], w=[('PX', b)])
                        P.op('act', lambda e, b=b, c=c: e.activation(CGT[:, c, 0:n], PX[b][:, 0:n], AF.Identity), r=[('PX', b)], w=[('CGT', c)])
                        b2 = pxc[0] % 2
                        pxc[0] += 1
                        for K in range(8):
                            P.op('pe', lambda e, b2=b2, K=K, c=c: e.matmul(PX[b2][:, 0:n], WI[:, K, 2688 + c * 128:2688 + (c + 1) * 128], H[:, K, 0:n],
                                                                           start=(K == 0), stop=(K == 7)), r=[('WI', K), HR], w=[('PX', b2)])
                        P.op('dve', lambda e, b2=b2, c=c: e.tensor_tensor(OS[:, 3, c, 0:n], CGT[:, c, 0:n], PX[b2][:, 0:n], ALU.mult),
                             r=[('PX', b2), ('CGT', c)], w=[('OS', 3)])
                    P.op('pool', lambda e, c0=c0, n=n: e.dma_start(out=fm(CSD)[:, :, c0:c0 + n], in_=OS[:, 3, :, 0:n]),
                         r=[('OS', 3)], dma='OSS3')

                norm_stage(XS, tiles[0][0], tiles[0][1], XL, SQ, RS, TMP, HH[:, 0], PSS, 1, 1 if tiles[0][2] else 0, hres=('H', 0))
                for ti, (c0, n, ic) in enumerate(tiles):
                    tile_body(ti, c0, n, ic, tiles[ti + 1] if ti + 1 < len(tiles) else None)
                P.flush()

        def phase_attn(l, KD, VA, with_ctx):
            with ExitStack() as ps:
                t_ = lambda name, shape, dt=F32: ps.enter_context(nc.sbuf_tensor(_u(name), list(shape), dt))
                QZ = t_("QZ", [128, 2, 8, 512], BF16)
                PT = t_("PT", [128, 3, 2, 512], BF16)
                OSB = t_("OSB", [128, 2, 512])
                RB = t_("RB", [128, 2, 512])
                PSs = [ps.enter_context(nc.psum_tensor(_u("PS%d" % i), [128, 2, 512], F32)) for i in range(3)]
                PO = [ps.enter_context(nc.psum_tensor(_u("PO%d" % i), [128, 512], F32)) for i in range(2)]
                P.op('dve', lambda e: e.memset(QZ[:], 0.0), w=[('QB', 0), ('QB', 1)])
                sc = [0]
                qtiles = [t for t in tiles if with_ctx or not t[2]]

                def head(h, qb, c0, n, kts):
                    g = h // 4
                    ob = h % 2
                    pairs = [kts[i:i + 2] for i in range(0, len(kts), 2)]
                    slots = []

                    def S(pr):
                        sl = sc[0] % 3
                        sc[0] += 1
                        slots.append(sl)
                        for jj, kt in enumerate(pr):
                            P.op('pe', lambda e, kt=kt, sl=sl, jj=jj: e.matmul(PSs[sl][:, jj, 0:n], KD[:, g, kt * 128:(kt + 1) * 128],
                                                                               QZ[:, qb, h, 0:n], start=True, stop=True),
                                 r=['KD', ('QB', qb)], w=[('PS', sl)])

                    LA = 2
                    for i in range(min(LA, len(pairs))):
                        S(pairs[i])
                    for i, pr in enumerate(pairs):
                        sl = slots[i]
                        P.op('act', lambda e, sl=sl, ng=len(pr): e.activation(PT[:, sl, 0:ng, 0:n], PSs[sl][:, 0:ng, 0:n], AF.Exp, scale=0.125),
                             r=[('PS', sl)], w=[('PT', sl)])
                        if i + LA < len(pairs):
                            S(pairs[i + LA])
                        for jj, kt in enumerate(pr):
                            first = (i == 0 and jj == 0)
                            lastm = (i == len(pairs) - 1 and jj == len(pr) - 1)
                            P.op('pe', lambda e, kt=kt, sl=sl, jj=jj, first=first, lastm=lastm: e.matmul(
                                PO[ob][0:65, 0:n], VA[:, kt, g, 0:65], PT[:, sl, jj, 0:n], start=first, stop=lastm),
                                r=['VA', 'VA1', ('PT', sl)], w=[('PO', ob)])
                    P.op('dve', lambda e: e.tensor_copy(OSB[0:65, ob, 0:n], PO[ob][0:65, 0:n]), r=[('PO', ob)], w=[('OSB', ob)])
                    P.op('dve', lambda e: e.reciprocal(OSB[64:65, ob, 0:n], OSB[64:65, ob, 0:n]), r=[('OSB', ob)], w=[('OSB', ob)])
                    P.op('pool', lambda e: e.dma_start(out=DEN[h:h + 1, c0:c0 + n], in_=OSB[64:65, ob, 0:n]),
                         r=[('OSB', ob)], w=[('DEND', ob)], dma='DENS%d' % ob)
                    P.op('pool', lambda e: e.dma_start(out=RB[0:64, ob, 0:n], in_=DEN[h:h + 1, c0:c0 + n].to_broadcast([64, n])),
                         r=[('DEND', ob)], w=[('RB', ob)], dma='RBL%d' % ob)
                    P.op('dve', lambda e: e.tensor_tensor(OSB[0:64, ob, 0:n], OSB[0:64, ob, 0:n], RB[0:64, ob, 0:n], ALU.mult),
                         r=[('OSB', ob), ('RB', ob)], w=[('OSB', ob)])
                    P.op('pool', lambda e: e.dma_start(out=ATTU[h * 64:(h + 1) * 64, c0:c0 + n], in_=OSB[0:64, ob, 0:n]),
                         r=[('OSB', ob)], dma='OSBS%d' % ob)

                def qload(qi):
                    c0, n, ic = qtiles[qi]
                    qb = qi % 2
                    for par in range(2):
                        P.op('sp', lambda e, par=par: e.dma_start(
                            out=QZ[par * 64:(par + 1) * 64, qb, par:8:2, 0:n], in_=fm(QT)[par * 64:(par + 1) * 64, :, c0:c0 + n]),
                            w=[('QB', qb)], dma='QB%d' % qb)

                qload(0)
                for qi, (c0, n, ic) in enumerate(qtiles):
                    qb = qi % 2
                    kts = list(range(CTX // 128)) if ic else list(range(NT128))
                    for h in range(8):
                        head(h, qb, c0, n, kts)
                        if h == 0 and qi + 1 < len(qtiles):
                            qload(qi + 1)
                P.flush()

        def phase_lrusc(l, with_ctx_out):
            with ExitStack() as ps:
                t_ = lambda name, shape, dt=F32: ps.enter_context(nc.sbuf_tensor(_u(name), list(shape), dt))
                LWB = t_("LWB", [128, 8, 128], BF16)
                UH = t_("UH", [128, 2, 2, 516])
                UC = t_("UC", [128, 2, 2, 512])
                UCB = t_("UCB", [128, 2, 2, 512], BF16)
                GR = t_("GR", [128, 2, 2, 512])
                GI = t_("GI", [128, 2, 2, 512])
                AA = t_("AA", [128, 2, 2, 512])
                A2 = t_("A2", [128, 2, 2, 512])
                HO = t_("HO", [128, 2, 2, 512])
                ST = t_("ST", [128, 2])
                CH = t_("CH", [128, 2, 2, 514])
                BGT = t_("BGT", [128, 2, 2, 512])
                SCT = t_("SCT", [128, 2, 2, 512])
                SC2 = t_("SC2", [128, 2, 2, 512])
                HFL = t_("HFL", [128, 2, 2, 512])
                GL = t_("GL", [128, 2, 2, 512])
                G2 = t_("G2", [128, 2, 2, 512])
                pt = lambda name: ps.enter_context(nc.psum_tensor(_u(name), [128, 512], F32))
                PR = [[pt("PR%d%d" % (p, c)) for c in range(2)] for p in range(2)]
                PI = [[pt("PI%d%d" % (p, c)) for c in range(2)] for p in range(2)]
                P.op('pool', lambda e: e.dma_start(out=LWB[:], in_=LW[l].rearrange("p (a b) -> p a b", b=128)), w=['LWB'], dma='W1')

                def seg(ic):
                    return (0, CTX) if ic else (CTX, TT)

                def lru_load(p, c0, n, ic):
                    lo, hi = seg(ic)
                    a = max(c0 - 2, lo)
                    bb = min(c0 + n + 1, hi)
                    if a > c0 - 2:
                        P.op('dve', lambda e: e.memset(UH[:, p, :, 0:2], 0.0), w=[('UH', p)])
                    if bb < c0 + n + 1:
                        P.op('dve', lambda e: e.memset(UH[:, p, :, n + 2:n + 3], 0.0), w=[('UH', p)])
                    P.op('sp', lambda e: e.dma_start(out=UH[:, p, :, a - (c0 - 2):bb - (c0 - 2)], in_=fm(UD)[:, :, a:bb]), w=[('UH', p)], dma='UH%d' % p)

                def lru_compute(d, p, c0, n, ic):
                    if d == 0:
                        for c in range(2):
                            wo = O_LCW + c * 4
                            P.op('act', lambda e, c=c, wo=wo: e.activation(UC[:, p, c, 0:n], UH[:, p, c, 0:n], AF.Identity, bias=smc(O_LCB + c), scale=smc(wo)),
                                 r=[('UH', p), 'SM'], w=[('UC', p)])
                            for j in range(1, 4):
                                P.op('dve', lambda e, c=c, wo=wo, j=j: e.scalar_tensor_tensor(UC[:, p, c, 0:n], UH[:, p, c, j:j + n], smc(wo + j), UC[:, p, c, 0:n],
                                                                                               ALU.mult, ALU.add), r=[('UH', p), 'SM', ('UC', p)], w=[('UC', p)])
                        P.op('pool', lambda e: e.dma_start(out=fm(UCD)[:, :, c0:c0 + n], in_=UC[:, p, :, 0:n]), r=[('UC', p)], dma='UCS%d' % p)
                    P.op('act', lambda e: e.activation(UCB[:, p, :, 0:n], UC[:, p, :, 0:n], AF.Identity), r=[('UC', p)], w=[('UCB', p)])
                    for c in range(2):
                        P.op('pe', lambda e, c=c: e.matmul(PR[p][c][:, 0:n], LWB[:, d * 4 + c, :], UCB[:, p, c, 0:n], start=True, stop=True),
                             r=['LWB', ('UCB', p)], w=[('PR', p, c)])
                        P.op('pe', lambda e, c=c: e.matmul(PI[p][c][:, 0:n], LWB[:, d * 4 + 2 + c, :], UCB[:, p, c, 0:n], start=True, stop=True),
                             r=['LWB', ('UCB', p)], w=[('PI', p, c)])
                        ob = O_LB + d * 4
                        P.op('act', lambda e, c=c, ob=ob: e.activation(GR[:, p, c, 0:n], PR[p][c][:, 0:n], AF.Sigmoid, bias=smc(ob + c)),
                             r=[('PR', p, c), 'SM'], w=[('GR', p, c)])
                        P.op('act', lambda e, c=c, ob=ob: e.activation(GI[:, p, c, 0:n], PI[p][c][:, 0:n], AF.Sigmoid, bias=smc(ob + 2 + c)),
                             r=[('PI', p, c), 'SM'], w=[('GI', p, c)])
                    for c in range(2):
                        P.op('act', lambda e, c=c: e.activation(AA[:, p, c, 0:n], GR[:, p, c, 0:n], AF.Exp, scale=C1[:, d * 2 + c:d * 2 + c + 1]),
                             r=[('GR', p, c), 'C1'], w=[('AA', p, c)])
                        P.op('act', lambda e, c=c: e.activation(A2[:, p, c, 0:n], GR[:, p, c, 0:n], AF.Exp, scale=C2[:, d * 2 + c:d * 2 + c + 1]),
                             r=[('GR', p, c), 'C1'], w=[('A2', p, c)])
                    a2r = [('A2', p, 0), ('A2', p, 1)]
                    gir = [('GI', p, 0), ('GI', p, 1)]
                    P.op('act', lambda e: e.activation(A2[:, p, :, 0:n], A2[:, p, :, 0:n], AF.Sqrt, bias=1.0, scale=-1.0), r=a2r, w=a2r)
                    P.op('dve', lambda e: e.tensor_tensor(GI[:, p, :, 0:n], GI[:, p, :, 0:n], UC[:, p, :, 0:n], ALU.mult), r=gir + [('UC', p)], w=gir)
                    P.op('dve', lambda e: e.tensor_tensor(GI[:, p, :, 0:n], GI[:, p, :, 0:n], A2[:, p, :, 0:n], ALU.mult), r=gir + a2r, w=gir)
                    for c in range(2):
                        if d == 0:
                            P.op('dve', lambda e, c=c: e.tensor_tensor_scan(HO[:, p, c, 0:n], AA[:, p, c, 0:n], GI[:, p, c, 0:n], ST[:, c:c + 1], ALU.mult, ALU.add),
                                 r=[('AA', p, c), ('GI', p, c), 'ST'], w=[('HO', p)])
                            P.op('dve', lambda e, c=c: e.tensor_copy(ST[:, c:c + 1], HO[:, p, c, n - 1:n]), r=[('HO', p)], w=['ST'])
                        else:
                            P.op('dve', lambda e, c=c: e.tensor_tensor_scan(rev(HO[:, p, c, 0:n], n), rev(AA[:, p, c, 0:n], n), rev(GI[:, p, c, 0:n], n),
                                                                            ST[:, c:c + 1], ALU.mult, ALU.add),
                                 r=[('AA', p, c), ('GI', p, c), 'ST'], w=[('HO', p)])
                            P.op('dve', lambda e, c=c: e.tensor_copy(ST[:, c:c + 1], HO[:, p, c, 0:1]), r=[('HO', p)], w=['ST'])

                def sc_load(p, c0, n, ic):
                    lo, hi = seg(ic)
                    a = max(c0 - 1, lo)
                    bb = min(c0 + n + 1, hi)
                    if a > c0 - 1:
                        P.op('dve', lambda e: e.memset(CH[:, p, :, 0:1], 0.0), w=[('CH', p)])
                    if bb < c0 + n + 1:
                        P.op('dve', lambda e: e.memset(CH[:, p, :, n + 1:n + 2], 0.0), w=[('CH', p)])
                    P.op('sp', lambda e: e.dma_start(out=CH[:, p, :, a - (c0 - 1):bb - (c0 - 1)], in_=fm(CSD)[:, :, a:bb]), w=[('CH', p)], dma='CH%d' % p)
                    P.op('sp', lambda e: e.dma_start(out=BGT[:, p, :, 0:n], in_=fm(BGD)[:, :, c0:c0 + n]), w=[('BGT', p)], dma='BGT%d' % p)

                def sc_compute(p, c0, n, ic):
                    for c in range(2):
                        wo = O_SCW + c * 3
                        P.op('act', lambda e, c=c, wo=wo: e.activation(SCT[:, p, c, 0:n], CH[:, p, c, 0:n], AF.Identity, bias=smc(O_SCB + c), scale=smc(wo)),
                             r=[('CH', p), 'SM'], w=[('SCT', p)])
                        for j in range(1, 3):
                            P.op('dve', lambda e, c=c, wo=wo, j=j: e.scalar_tensor_tensor(SCT[:, p, c, 0:n], CH[:, p, c, j:j + n], smc(wo + j), SCT[:, p, c, 0:n],
                                                                                           ALU.mult, ALU.add), r=[('CH', p), 'SM', ('SCT', p)], w=[('SCT', p)])
                    P.op('dve', lambda e: e.tensor_tensor(SCT[:, p, :, 0:n], SCT[:, p, :, 0:n], BGT[:, p, :, 0:n], ALU.mult), r=[('SCT', p), ('BGT', p)], w=[('SCT', p)])
                    P.op('pool', lambda e: e.dma_start(out=fm(SCD)[:, :, c0:c0 + n], in_=SCT[:, p, :, 0:n]), r=[('SCT', p)], dma='SCS%d' % p)

                P.op('dve', lambda e: e.memset(ST[:], 0.0), w=['ST'])
                order = list(tiles)
                lru_load(0, *order[0])
                for ti, (c0, n, ic) in enumerate(order):
                    p = ti % 2
                    do_sc = not (ic and not with_ctx_out)
                    if do_sc:
                        sc_load(p, c0, n, ic)
                    if ti + 1 < len(order):
                        lru_load((ti + 1) % 2, *order[ti + 1])
                    lru_compute(0, p, c0, n, ic)
                    P.op('pool', lambda e, p=p, c0=c0, n=n: e.dma_start(out=fm(HF)[:, :, c0:c0 + n], in_=HO[:, p, :, 0:n]), r=[('HO', p)], dma='HOS%d' % p)
                    if do_sc:
                        sc_compute(p, c0, n, ic)
                P.flush()
                P.op('dve', lambda e: e.memset(ST[:], 0.0), w=['ST'])
                order = [tiles[0]] + tiles[:0:-1]

                def lru_load(p, c0, n, ic):
                    P.op('sp', lambda e: e.dma_start(out=UC[:, p, :, 0:n], in_=fm(UCD)[:, :, c0:c0 + n]), w=[('UC', p)], dma='UH%d' % p)

                lru_load(0, *order[0])

                def comb_load(p, c0, n):
                    P.op('sp', lambda e: e.dma_start(out=HFL[:, p, :, 0:n], in_=fm(HF)[:, :, c0:c0 + n]), w=[('HFL', p)], dma='HFL%d' % p)
                    P.op('sp', lambda e: e.dma_start(out=GL[:, p, :, 0:n], in_=fm(GD)[:, :, c0:c0 + n]), w=[('GL', p)], dma='GL%d' % p)

                def gate(p, n):
                    P.op('act', lambda e: e.activation(G2[:, p, :, 0:n], GL[:, p, :, 0:n], AF.Square, scale=0.21145921592590907), r=[('GL', p)], w=[('G2', p)])
                    P.op('dve', lambda e: e.scalar_tensor_tensor(G2[:, p, :, 0:n], G2[:, p, :, 0:n], 1.0, GL[:, p, :, 0:n], ALU.add, ALU.mult),
                         r=[('G2', p), ('GL', p)], w=[('G2', p)])
                    P.op('act', lambda e: e.activation(G2[:, p, :, 0:n], G2[:, p, :, 0:n], AF.Sigmoid, scale=1.5957691216057308), r=[('G2', p)], w=[('G2', p)])
                    P.op('dve', lambda e: e.tensor_tensor(G2[:, p, :, 0:n], G2[:, p, :, 0:n], GL[:, p, :, 0:n], ALU.mult), r=[('G2', p), ('GL', p)], w=[('G2', p)])

                for ti, (c0, n, ic) in enumerate(order):
                    p = ti % 2
                    do_out = not (ic and not with_ctx_out)
                    if do_out:
                        comb_load(p, c0, n)
                    if ti + 1 < len(order):
                        lru_load((ti + 1) % 2, *order[ti + 1])
                    if do_out:
                        gate(p, n)
                    lru_compute(1, p, c0, n, ic)
                    if do_out:
                        P.op('dve', lambda e, p=p, n=n: e.tensor_tensor(HFL[:, p, :, 0:n], HFL[:, p, :, 0:n], HO[:, p, :, 0:n], ALU.add),
                             r=[('HFL', p), ('HO', p)], w=[('HFL', p)])
                        P.op('dve', lambda e, p=p, n=n: e.tensor_tensor(HFL[:, p, :, 0:n], HFL[:, p, :, 0:n], G2[:, p, :, 0:n], ALU.mult),
                             r=[('HFL', p), ('G2', p)], w=[('HFL', p)])
                        P.op('pool', lambda e, p=p, c0=c0, n=n: e.dma_start(out=fm(LRD)[:, :, c0:c0 + n], in_=HFL[:, p, :, 0:n]), r=[('HFL', p)], dma='HFS%d' % p)
                P.flush()

        def phase_mix_out(l, do_ctx):
            tl = [t for t in tiles if do_ctx or not t[2]]
            with ExitStack() as ps:
                t_ = lambda name, shape, dt=F32: ps.enter_context(nc.sbuf_tensor(_u(name), list(shape), dt))
                WO = t_("WO", [128, 8, D], BF16)
                MX = t_("MX", [128, 2, 8, 512])
                SQ8 = t_("SQ8", [128, 8, 512], BF16)
                RG = t_("RG", [128, 3, 512])
                TMP = t_("TMP", [128, 2, 512])
                M = t_("M", [128, 2, 8, 512], BF16)
                XR = t_("XR", [128, 3, 512])
                pt = lambda name: ps.enter_context(nc.psum_tensor(_u(name), [128, 512], F32))
                PG = [pt("PGa"), pt("PGb"), pt("PGc")]
                PO = [pt("PO0"), pt("PO1")]
                wv = WOUT[l].rearrange("(k p) c -> p k c", p=128)
                P.op('pool', lambda e: e.dma_start(out=WO[:], in_=wv), w=['WO'], dma='W1')
                xrc = [0]
                grp = [0, 0, 0, 0, 1, 1, 2, 2]
                gw = [512.0, 256.0, 256.0]

                def loads(ti):
                    c0, n, ic = tl[ti]
                    p = ti % 2
                    P.op('sp', lambda e: e.dma_start(out=MX[:, p, 0:4, 0:n], in_=fm(ATTU)[:, :, c0:c0 + n]), w=[('MX', p, 0)], dma='MX0%d' % p)
                    P.op('sp', lambda e: e.dma_start(out=MX[:, p, 4:6, 0:n], in_=fm(LRD)[:, :, c0:c0 + n]), w=[('MX', p, 1)], dma='MX1%d' % p)
                    P.op('sp', lambda e: e.dma_start(out=MX[:, p, 6:8, 0:n], in_=fm(SCD)[:, :, c0:c0 + n]), w=[('MX', p, 2)], dma='MX2%d' % p)

                def frontA1(ti):
                    c0, n, ic = tl[ti]
                    p = ti % 2
                    for k in range(8):
                        g = grp[k]
                        P.op('act', lambda e, k=k: e.activation(SQ8[:, k, 0:n], MX[:, p, k, 0:n], AF.Square), r=[('MX', p, g)], w=[('SQ8', k)])

                def frontA2(ti):
                    c0, n, ic = tl[ti]
                    for k in range(8):
                        g = grp[k]
                        first = k in (0, 4, 6)
                        lastk = k in (3, 5, 7)
                        P.op('pe', lambda e, k=k, g=g, first=first, lastk=lastk: e.matmul(PG[g][:, 0:n], ONES[:], SQ8[:, k, 0:n], start=first, stop=lastk),
                             r=[('SQ8', k), 'ONES'], w=[('PG', g)])
                    for g in range(3):
                        P.op('act', lambda e, g=g: e.activation(RG[:, g, 0:n], PG[g][:, 0:n], AF.Ln, bias=EPSV[:, 0:1], scale=1.0 / gw[g]), r=[('PG', g)], w=[('RG', g)])
                        P.op('act', lambda e, g=g: e.activation(RG[:, g, 0:n], RG[:, g, 0:n], AF.Exp, scale=-0.5), r=[('RG', g)], w=[('RG', g)])

                def frontC(ti, ks):
                    c0, n, ic = tl[ti]
                    p = ti % 2
                    for k in ks:
                        g = grp[k]
                        P.op('dve', lambda e, k=k, g=g: e.tensor_tensor(TMP[:, k % 2, 0:n], MX[:, p, k, 0:n], RG[:, g, 0:n], ALU.mult),
                             r=[('MX', p, g), ('RG', g)], w=[('TMP', k % 2)])
                        P.op('act', lambda e, k=k: e.activation(M[:, p, k, 0:n], TMP[:, k % 2, 0:n], AF.Identity, scale=smc(O_GG + k)),
                             r=[('TMP', k % 2), 'SM'], w=[('M', p)])

                def back(ti, ks):
                    c0, n, ic = tl[ti]
                    p = ti % 2
                    s = 1 if ic else 0
                    for k in ks:
                        b = xrc[0] % 3
                        xrc[0] += 1
                        P.op('sp', lambda e, k=k, b=b: e.dma_start(out=XR[:, b, 0:n], in_=XS[k * 128:(k + 1) * 128, c0:c0 + n]),
                             w=[('XR', b)], dma='XR%d' % b)
                        for j in range(8):
                            P.op('pe', lambda e, k=k, j=j: e.matmul(PO[k % 2][:, 0:n], WO[:, j, k * 128:(k + 1) * 128], M[:, p, j, 0:n],
                                                                    start=(j == 0), stop=(j == 7)), r=['WO', ('M', p)], w=[('PO', k % 2)])
                        P.op('dve', lambda e, k=k, b=b: e.scalar_tensor_tensor(XR[:, b, 0:n], PO[k % 2][:, 0:n], GT[:, 1, k, s:s + 1],
                                                                               XR[:, b, 0:n], ALU.mult, ALU.add),
                             r=[('PO', k % 2), ('XR', b), 'MOD'], w=[('XR', b)])
                        P.op('pool', lambda e, k=k, b=b: e.dma_start(out=XS[k * 128:(k + 1) * 128, c0:c0 + n], in_=XR[:, b, 0:n]),
                             r=[('XR', b)], dma='XRS%d' % b)

                loads(0)
                if len(tl) > 1:
                    loads(1)
                frontA1(0)
                frontA2(0)
                frontC(0, range(8))
                for ti in range(len(tl)):
                    nx = ti + 1 < len(tl)
                    back(ti, [0])
                    if nx:
                        frontA1(ti + 1)
                    back(ti, [1, 2])
                    if nx:
                        frontA2(ti + 1)
                    back(ti, [3, 4])
                    if nx:
                        frontC(ti + 1, range(0, 4))
                    back(ti, [5, 6])
                    if nx:
                        frontC(ti + 1, range(4, 8))
                    back(ti, [7])
                    if ti + 2 < len(tl):
                        loads(ti + 2)
                P.flush()

        def phase_final():
            with ExitStack() as ps:
                t_ = lambda name, shape, dt=F32: ps.enter_context(nc.sbuf_tensor(_u(name), list(shape), dt))
                XL = t_("XL", [128, 2, 8, 512])
                SQ = t_("SQ", [128, 2, 512], BF16)
                RS = t_("RS", [128, 512])
                TMP = t_("TMP", [128, 2, 512])
                YO = t_("YO", [128, 2, 8, 512])
                PSS = ps.enter_context(nc.psum_tensor(_u("PSS"), [128, 512], F32))
                tl = [t for t in tiles if not t[2]]

                def load(ti):
                    c0, n, ic = tl[ti]
                    p = ti % 2
                    P.op('sp', lambda e: e.dma_start(out=XL[:, p, :, 0:n], in_=fm(XS)[:, :, c0:c0 + n]), w=[('XL', p)], dma='XL%d' % p)

                def comp(ti):
                    c0, n, ic = tl[ti]
                    p = ti % 2
                    for k in range(8):
                        P.op('act', lambda e, k=k: e.activation(SQ[:, k % 2, 0:n], XL[:, p, k, 0:n], AF.Square), r=[('XL', p)], w=[('SQ', k % 2)])
                        P.op('pe', lambda e, k=k: e.matmul(PSS[:, 0:n], ONES[:], SQ[:, k % 2, 0:n], start=(k == 0), stop=(k == 7)),
                             r=[('SQ', k % 2), 'ONES'], w=['PSS'])
                    P.op('act', lambda e: e.activation(RS[:, 0:n], PSS[:, 0:n], AF.Ln, bias=EPSV[:, 0:1], scale=1.0 / D), r=['PSS'], w=['RS'])
                    P.op('act', lambda e: e.activation(RS[:, 0:n], RS[:, 0:n], AF.Exp, scale=-0.5), r=['RS'], w=['RS'])
                    for k in range(8):
                        P.op('dve', lambda e, k=k: e.tensor_tensor(TMP[:, k % 2, 0:n], XL[:, p, k, 0:n], RS[:, 0:n], ALU.mult),
                             r=[('XL', p), 'RS'], w=[('TMP', k % 2)])
                        P.op('act', lambda e, k=k: e.activation(YO[:, p, k, 0:n], TMP[:, k % 2, 0:n], AF.Identity, scale=smc(O_FG + k)),
                             r=[('TMP', k % 2), 'SM'], w=[('YO', p)])
                    P.op('pool', lambda e: e.dma_start(out=fm(Y)[:, :, c0 - CTX:c0 - CTX + n], in_=YO[:, p, :, 0:n]), r=[('YO', p)], dma='YOS%d' % p)

                load(0)
                for ti in range(len(tl)):
                    if ti + 1 < len(tl):
                        load(ti + 1)
                    comp(ti)
                P.flush()

        cnt = [0]

        def go():
            cnt[0] += 1
            return (stop is None or cnt[0] <= stop) and cnt[0] >= start

        for l in range(DEPTH):
            last = (l == DEPTH - 1)
            if go():
                phase_mod(l)
            if go():
                phase_ffn(l, 0, XT0 if l == 0 else XS, XS, True)
            with nc.sbuf_tensor(_u("KD"), [128, 2, TT], BF16) as KD, nc.sbuf_tensor(_u("VA"), [128, NT128, 2, 65], BF16) as VA:
                if go():
                    phase_mix_in(l, KD, VA)
                if go():
                    phase_attn(l, KD, VA, not last)
            if go():
                phase_lrusc(l, not last)
            if go():
                phase_mix_out(l, not last)
            if go():
                phase_ffn(l, 1, XS, XS, not last)
        phase_final()
        _DBG['P'] = P
    return nc


def _partner(d):
    w = (d % 32) // 16
    return d + 16 if w == 0 else d - 16


def prep_shared(inp, SEQ):
    f = np.float32
    TT = CTX + SEQ
    small = np.zeros((DEPTH, 128, NS), f)
    win = np.zeros((DEPTH, D, WIN_COLS), f)
    lw = np.zeros((DEPTH, 128, 8, 128), f)
    dd = np.arange(64)
    part = np.array([_partner(d) for d in dd])
    for l in range(DEPTH):
        small[l, :, O_BM:O_BM + 72] = inp['b_mod'][l].reshape(72, 128).T
        small[l, :, O_NG:O_NG + 24] = inp['norm_g'][l].reshape(24, 128).T
        qg = inp['q_norm_g'][l]
        kg = inp['k_norm_g'][l]
        small[l, :, O_GQK + 0] = np.tile(qg, 2)
        small[l, :, O_GQK + 1] = np.tile(qg[part], 2)
        small[l, :, O_GQK + 2] = np.tile(kg, 2)
        small[l, :, O_GQK + 3] = np.tile(kg[part], 2)
        small[l, :, O_LCW:O_LCW + 8] = inp['lru_conv_w'][l].reshape(4, 2, 128).transpose(2, 1, 0).reshape(128, 8)
        small[l, :, O_LCB:O_LCB + 2] = inp['lru_conv_b'][l].reshape(2, 128).T
        for d in range(2):
            small[l, :, O_LB + d * 4 + 0:O_LB + d * 4 + 2] = inp['lru_ba'][l, d].reshape(2, 128).T
            small[l, :, O_LB + d * 4 + 2:O_LB + d * 4 + 4] = inp['lru_bx'][l, d].reshape(2, 128).T
            small[l, :, O_LAM + d * 2:O_LAM + d * 2 + 2] = inp['lru_lambda'][l, d].reshape(2, 128).T
            for ax, wn in enumerate(('lru_wa', 'lru_wx')):
                for c in range(2):
                    for hb in range(2):
                        lw[l, hb * 64:(hb + 1) * 64, d * 4 + ax * 2 + c, hb * 64:(hb + 1) * 64] = inp[wn][l, d, 2 * c + hb]
        small[l, :, O_SCW:O_SCW + 6] = inp['sc_conv_w'][l].reshape(3, 2, 128).transpose(2, 1, 0).reshape(128, 6)
        small[l, :, O_SCB:O_SCB + 2] = inp['sc_conv_b'][l].reshape(2, 128).T
        small[l, :, O_GG:O_GG + 8] = inp['grp_norm_g'][l].reshape(8, 128).T
        small[l, :, O_FG:O_FG + 8] = inp['final_norm_g'].reshape(8, 128).T
        w = inp['w_in'][l]
        q = w[:, 0:512]
        k = w[:, 512:640]
        hp = (np.arange(512) // 64) * 64 + part[np.arange(512) % 64]
        kp = (np.arange(128) // 64) * 64 + part[np.arange(128) % 64]
        q2 = q[:, hp]
        k2 = k[:, kp]
        win[l, :, 0:512] = q
        win[l, :, 512:640] = np.concatenate([k[:, 0:64], k[:, 0:64]], 1)
        win[l, :, 640:768] = np.concatenate([k[:, 64:128], k[:, 64:128]], 1)
        win[l, :, 768:1280] = q2
        win[l, :, 1280:1408] = np.concatenate([k2[:, 0:64], k2[:, 0:64]], 1)
        win[l, :, 1408:1536] = np.concatenate([k2[:, 64:128], k2[:, 64:128]], 1)
        win[l, :, 1536:1664] = w[:, 640:768]
        win[l, :, 1664:2944] = w[:, 768:2048]
    rows = SEQ // 64
    row_ids = np.repeat(np.arange(rows), 64).astype(f)
    col_ids = np.tile(np.arange(64), rows).astype(f)
    inv = (np.float32(10000.0) ** (-np.arange(16, dtype=f) / np.float32(16))).astype(f)
    rope = np.zeros((128, 2, TT), f)
    rope[:, 0, :CTX] = 1.0
    for p in range(128):
        d = p % 64
        half = d // 32
        j = d % 16
        wsel = (d % 32) // 16
        ang = (row_ids if half == 0 else col_ids) * inv[j]
        rope[p, 0, CTX:] = np.cos(ang)
        sn = np.sin(ang)
        rope[p, 1, CTX:] = -sn if wsel == 0 else sn
    return dict(small=small, w_inp=win, lru_w=lw.reshape(DEPTH, 128, 8 * 128), rope=rope,
                w_mod=np.ascontiguousarray(inp['w_mod'], f), w_ffn_in=np.ascontiguousarray(inp['w_ffn_in'], f),
                w_ffn_out=np.ascontiguousarray(inp['w_ffn_out'], f), w_out=np.ascontiguousarray(inp['w_out'], f))


def run(inp, debug=False, n_cores=None, stop=None, start=0):
    x = np.asarray(inp['x'])
    B, SEQ, _ = x.shape
    inp = {k: np.asarray(v, np.float32) for k, v in inp.items()}
    shared = prep_shared(inp, SEQ)
    nc = build(SEQ, debug=debug, stop=stop, start=start)
    in_maps = []
    for b in range(B):
        m = dict(shared)
        m['xt0'] = np.ascontiguousarray(np.concatenate([inp['ctx'][b].T, inp['x'][b].T], axis=1))
        cv = np.stack([inp['c'][b].reshape(8, 128).T, inp['c_ctx'].reshape(8, 128).T], axis=-1)
        m['cvec'] = np.ascontiguousarray(cv, np.float32)
        in_maps.append(m)
    res = run_bass_kernel_spmd(nc, in_maps, core_ids=list(range(B)))
    out = np.stack([np.ascontiguousarray(r['y'].T) for r in res.results], axis=0).astype(np.float32)
    if debug:
        return out, res.results
    return out


def kernel(**inputs):
    return run(inputs)
```
